# Optimizing a Trainium2 kernel written in Bass

```python
import math
import jax, jax.numpy as jnp
from jax import lax
import numpy as np

D_MODEL = 2048
BATCH = 4
SEQ = 4096
DEPTH = 4

N_MEM = 256
N_MIXERS = 3
BRANCH_WIDTH = D_MODEL
MEM_HEADS = 4
MEM_HEAD_DIM = 128
MEM_WIDTH = MEM_HEADS * MEM_HEAD_DIM
MIX_WIDTH = BRANCH_WIDTH - MEM_WIDTH

SWA_HEAD_DIM = 64
SWA_Q_HEADS = MIX_WIDTH // SWA_HEAD_DIM
SWA_KV_HEADS = SWA_Q_HEADS // 8
SWA_WINDOW = 128

MOBA_HEAD_DIM = 128
MOBA_HEADS = MIX_WIDTH // MOBA_HEAD_DIM
MOBA_BLOCK = 256
MOBA_TOPK = 3
MOBA_Q_CHUNK = 32

RET_HEADS = 6
RET_V_DIM = MIX_WIDTH // RET_HEADS
RET_QK_DIM = RET_V_DIM // 2
RET_CHUNK = 128
RET_THETA = 10000.0

ROPE_THETA = 500000.0
ROPE_FRACTION = 4
EPS = 1e-6

IN_COLS = (
    SWA_Q_HEADS * SWA_HEAD_DIM + 2 * SWA_KV_HEADS * SWA_HEAD_DIM + MEM_WIDTH + BRANCH_WIDTH,
    3 * MIX_WIDTH + MEM_WIDTH + BRANCH_WIDTH,
    2 * RET_HEADS * RET_QK_DIM + RET_HEADS * RET_V_DIM + MEM_WIDTH + BRANCH_WIDTH,
)

kernel_name = "hybrid_swa_moba_retention_trunk"

F32 = jnp.float32


def rmsnorm(x, g):
    xf = x.astype(F32)
    y = xf * lax.rsqrt(jnp.mean(xf * xf, axis=-1, keepdims=True) + EPS)
    return (y * g.astype(F32)).astype(x.dtype)


def rope_angles(positions, rot_dim, theta):
    inv = theta ** (-jnp.arange(0, rot_dim, 2, dtype=F32) / rot_dim)
    ang = positions.astype(F32)[..., None] * inv
    return jnp.cos(ang)[:, :, None, :], jnp.sin(ang)[:, :, None, :]


def apply_rope(x, cos, sin):
    r = 2 * cos.shape[-1]
    xf = x[..., :r].astype(F32)
    x1, x2 = xf[..., : r // 2], xf[..., r // 2:]
    rot = jnp.concatenate([x1 * cos - x2 * sin, x2 * cos + x1 * sin], axis=-1).astype(x.dtype)
    return jnp.concatenate([rot, x[..., r:]], axis=-1)


def swa_sink_attention(q, k, v, sinks):
    B, S, Hq, dh = q.shape
    Hkv = k.shape[2]
    G = Hq // Hkv
    W = SWA_WINDOW
    nb = S // W
    qb = q.reshape(B, nb, W, Hkv, G, dh)

    def with_prev(t):
        tb = t.reshape(B, nb, W, Hkv, dh)
        prev = jnp.pad(tb, ((0, 0), (1, 0), (0, 0), (0, 0), (0, 0)))[:, :-1]
        return jnp.concatenate([prev, tb], axis=2)

    kb, vb = with_prev(k), with_prev(v)
    s = jnp.einsum('bnqhgd,bnkhd->bnhgqk', qb, kb, preferred_element_type=F32) * (dh ** -0.5)
    qi = jnp.arange(W)[:, None] + W
    ki = jnp.arange(2 * W)[None, :]
    blk = jnp.arange(nb)[:, None, None]
    valid = (ki <= qi) & (ki > qi - W) & ((blk > 0) | (ki >= W))
    s = jnp.where(valid[None, :, None, None], s, -jnp.inf)
    sink = sinks.astype(F32).reshape(1, 1, Hkv, G, 1, 1)
    m = jnp.maximum(jnp.max(s, axis=-1, keepdims=True), sink)
    p = jnp.exp(s - m)
    p = p / (jnp.sum(p, axis=-1, keepdims=True) + jnp.exp(sink - m))
    o = jnp.einsum('bnhgqk,bnkhd->bnqhgd', p.astype(v.dtype), vb)
    return o.reshape(B, S, Hq * dh)


def moba_attention(q, k, v):
    B, S, H, dh = q.shape
    L = MOBA_BLOCK
    C = MOBA_Q_CHUNK
    Sp = -(-S // L) * L
    nblk = Sp // L
    pad = ((0, 0), (0, Sp - S), (0, 0), (0, 0))
    qh = jnp.pad(q, pad).transpose(0, 2, 1, 3)
    kb = jnp.pad(k, pad).transpose(0, 2, 1, 3).reshape(B, H, nblk, L, dh)
    vb = jnp.pad(v, pad).transpose(0, 2, 1, 3).reshape(B, H, nblk, L, dh)
    scale = dh ** -0.5
    q_blk = jnp.arange(Sp) // L
    n_sel = min(MOBA_TOPK, nblk - 1)
    if n_sel > 0:
        kmean = jnp.mean(kb.astype(F32), axis=3)
        gate = jnp.einsum('bhsd,bhnd->bhsn', qh.astype(F32), kmean)
        past = jnp.arange(nblk)[None, :] < q_blk[:, None]
        gate = jnp.where(past, gate, -jnp.inf)
        _, sel = lax.top_k(gate, n_sel)
        sel_valid = sel < q_blk[:, None]
        bi = jnp.arange(B)[:, None, None, None]
        hi = jnp.arange(H)[None, :, None, None]

    def chunk_fn(c):
        start = c * C
        qc = lax.dynamic_slice_in_dim(qh, start, C, axis=2)
        own = start // L
        k_own = lax.dynamic_index_in_dim(kb, own, axis=2, keepdims=False)
        v_own = lax.dynamic_index_in_dim(vb, own, axis=2, keepdims=False)
        pos = start + jnp.arange(C)
        own_mask = (own * L + jnp.arange(L))[None, :] <= pos[:, None]
        s_own = jnp.einsum('bhqd,bhkd->bhqk', qc, k_own, preferred_element_type=F32) * scale
        s_own = jnp.where(own_mask, s_own, -jnp.inf)
        if n_sel > 0:
            sel_c = lax.dynamic_slice_in_dim(sel, start, C, axis=2)
            val_c = lax.dynamic_slice_in_dim(sel_valid, start, C, axis=2)
            k_sel = kb[bi, hi, sel_c]
            v_sel = vb[bi, hi, sel_c]
            s_sel = jnp.einsum('bhqd,bhqnkd->bhqnk', qc, k_sel, preferred_element_type=F32) * scale
            s_sel = jnp.where(val_c[..., None], s_sel, -jnp.inf).reshape(B, H, C, n_sel * L)
            p = jax.nn.softmax(jnp.concatenate([s_sel, s_own], axis=-1), axis=-1).astype(v.dtype)
            p_sel = p[..., : n_sel * L].reshape(B, H, C, n_sel, L)
            o = (jnp.einsum('bhqnk,bhqnkd->bhqd', p_sel, v_sel)
                 + jnp.einsum('bhqk,bhkd->bhqd', p[..., n_sel * L:], v_own))
        else:
            p = jax.nn.softmax(s_own, axis=-1).astype(v.dtype)
            o = jnp.einsum('bhqk,bhkd->bhqd', p, v_own)
        return o

    out = lax.map(chunk_fn, jnp.arange(Sp // C))
    out = out.transpose(1, 0, 3, 2, 4).reshape(B, Sp, H * dh)
    return out[:, :S]


def retention(q, k, v):
    B, S, H, dk = q.shape
    dv = v.shape[-1]
    T = RET_CHUNK
    nc = S // T
    log_g = jnp.log1p(-jnp.exp(jnp.linspace(math.log(1.0 / 32), math.log(1.0 / 512), H, dtype=F32)))
    k = k * (dk ** -0.5)
    qc = q.reshape(B, nc, T, H, dk)
    kc = k.reshape(B, nc, T, H, dk)
    vc = v.reshape(B, nc, T, H, dv)
    i = jnp.arange(T, dtype=F32)
    diff = i[:, None] - i[None, :]
    decay = jnp.where(diff >= 0, jnp.exp(jnp.maximum(diff, 0.0)[None] * log_g[:, None, None]), 0.0)
    s = jnp.einsum('bnqhd,bnkhd->bnhqk', qc, kc, preferred_element_type=F32) * decay[None, None]
    inner = jnp.einsum('bnhqk,bnkhe->bnqhe', s.astype(v.dtype), vc).astype(F32)
    zeta = jnp.exp((T - 1 - i)[:, None] * log_g[None, :])
    kv = jnp.einsum('bnkhd,bnkhe->bnhde', kc.astype(F32) * zeta[:, :, None], vc.astype(F32))
    g_chunk = jnp.exp(T * log_g)[None, :, None, None]

    def step(R, kv_n):
        return g_chunk * R + kv_n, R

    _, R_prev = lax.scan(step, jnp.zeros((B, H, dk, dv), F32), kv.transpose(1, 0, 2, 3, 4))
    xi = jnp.exp((i + 1)[:, None] * log_g[None, :])
    cross = jnp.einsum('bnqhd,nbhde->bnqhe', qc.astype(F32) * xi[:, :, None], R_prev)
    o = inner + cross
    o = o * lax.rsqrt(jnp.mean(o * o, axis=-1, keepdims=True) + EPS)
    return o.reshape(B, S, H * dv).astype(v.dtype)


def memory_attention(qm, mem_k, mem_v):
    B, S, Hm, dm = qm.shape
    s = jnp.einsum('bshd,bnhd->bhsn', qm, mem_k, preferred_element_type=F32) * (dm ** -0.5)
    p = jax.nn.softmax(s, axis=-1).astype(mem_v.dtype)
    return jnp.einsum('bhsn,bnhd->bshd', p, mem_v).reshape(B, S, Hm * dm)


def hybrid_layer(x, mixer_id, g, w_in, w_out, sinks, rope_a, rope_b, rope_c, mem_k, mem_v):
    B, S, _ = x.shape
    h = rmsnorm(x, g)
    proj = h @ w_in
    n_mix = w_in.shape[1] - MEM_WIDTH - BRANCH_WIDTH
    mix_in = proj[..., :n_mix]
    qm = proj[..., n_mix:n_mix + MEM_WIDTH].reshape(B, S, MEM_HEADS, MEM_HEAD_DIM)
    z = proj[..., n_mix + MEM_WIDTH:]
    if mixer_id == 0:
        nq = SWA_Q_HEADS * SWA_HEAD_DIM
        nkv = SWA_KV_HEADS * SWA_HEAD_DIM
        q = mix_in[..., :nq].reshape(B, S, SWA_Q_HEADS, SWA_HEAD_DIM)
        k = mix_in[..., nq:nq + nkv].reshape(B, S, SWA_KV_HEADS, SWA_HEAD_DIM)
        v = mix_in[..., nq + nkv:].reshape(B, S, SWA_KV_HEADS, SWA_HEAD_DIM)
        q, k = apply_rope(q, *rope_a), apply_rope(k, *rope_a)
        mix_out = swa_sink_attention(q, k, v, sinks)
    elif mixer_id == 1:
        q, k, v = [t.reshape(B, S, MOBA_HEADS, MOBA_HEAD_DIM) for t in jnp.split(mix_in, 3, axis=-1)]
        q, k = apply_rope(q, *rope_b), apply_rope(k, *rope_b)
        mix_out = moba_attention(q, k, v)
    else:
        nqk = RET_HEADS * RET_QK_DIM
        q = mix_in[..., :nqk].reshape(B, S, RET_HEADS, RET_QK_DIM)
        k = mix_in[..., nqk:2 * nqk].reshape(B, S, RET_HEADS, RET_QK_DIM)
        v = mix_in[..., 2 * nqk:].reshape(B, S, RET_HEADS, RET_V_DIM)
        q, k = apply_rope(q, *rope_c), apply_rope(k, *rope_c)
        mix_out = retention(q, k, v)
    mem_out = memory_attention(qm, mem_k, mem_v)
    y = jnp.concatenate([mix_out, mem_out], axis=-1) * jax.nn.silu(z)
    return x + y @ w_out


def setup_inputs(seed: int = 0) -> dict:
    key = jax.random.key(seed)
    keys = jax.random.split(key, 4 + 4 * DEPTH + 1)
    out = {}
    out['x'] = jax.random.normal(keys[0], (BATCH, SEQ, D_MODEL), F32)
    out['mem'] = jax.random.normal(keys[1], (BATCH, N_MEM, D_MODEL), F32)
    out['positions'] = jnp.broadcast_to(jnp.arange(SEQ, dtype=jnp.int32), (BATCH, SEQ))
    out['mem_norm'] = 1.0 + 0.02 * jax.random.normal(keys[2], (D_MODEL,), F32)
    out['w_mem_kv'] = jax.random.normal(keys[3], (D_MODEL, 2 * MEM_WIDTH), F32) * D_MODEL ** -0.5
    for i in range(DEPTH):
        kg, ki, ks, ko = keys[4 + 4 * i: 8 + 4 * i]
        mid = i % N_MIXERS
        out[f'norm_{i}'] = 1.0 + 0.02 * jax.random.normal(kg, (D_MODEL,), F32)
        out[f'w_in_{i}'] = jax.random.normal(ki, (D_MODEL, IN_COLS[mid]), F32) * D_MODEL ** -0.5
        if mid == 0:
            out[f'sinks_{i}'] = jax.random.normal(ks, (SWA_Q_HEADS,), F32)
        out[f'w_out_{i}'] = jax.random.normal(ko, (BRANCH_WIDTH, D_MODEL), F32) * (0.5 * BRANCH_WIDTH ** -0.5)
    out['final_norm'] = 1.0 + 0.02 * jax.random.normal(keys[-1], (D_MODEL,), F32)
    return out


def reference(x, mem, positions, mem_norm, w_mem_kv,
              norm_0, w_in_0, sinks_0, w_out_0,
              norm_1, w_in_1, w_out_1,
              norm_2, w_in_2, w_out_2,
              norm_3, w_in_3, sinks_3, w_out_3,
              final_norm):
    B = x.shape[0]
    layers = [(norm_0, w_in_0, w_out_0, sinks_0),
              (norm_1, w_in_1, w_out_1, None),
              (norm_2, w_in_2, w_out_2, None),
              (norm_3, w_in_3, w_out_3, sinks_3)]
    mkv = rmsnorm(mem, mem_norm) @ w_mem_kv
    mem_k = mkv[..., :MEM_WIDTH].reshape(B, N_MEM, MEM_HEADS, MEM_HEAD_DIM)
    mem_v = mkv[..., MEM_WIDTH:].reshape(B, N_MEM, MEM_HEADS, MEM_HEAD_DIM)
    rope_a = rope_angles(positions, SWA_HEAD_DIM // ROPE_FRACTION, ROPE_THETA)
    rope_b = rope_angles(positions, MOBA_HEAD_DIM // ROPE_FRACTION, ROPE_THETA)
    rope_c = rope_angles(positions, RET_QK_DIM, RET_THETA)
    h = x
    for i in range(DEPTH):
        g, w_in, w_out, sinks = layers[i]
        h = hybrid_layer(h, i % N_MIXERS, g, w_in, w_out, sinks, rope_a, rope_b, rope_c, mem_k, mem_v)
    return rmsnorm(h, final_norm)
```

```python
import contextlib
import math
import os
import numpy as np
import ml_dtypes
import concourse.bass as bass
import concourse.mybir as mybir
from concourse.bass_utils import run_bass_kernel_spmd

F32 = mybir.dt.float32
BF16 = mybir.dt.bfloat16
I32 = mybir.dt.int32
AF = mybir.ActivationFunctionType
ALU = mybir.AluOpType
AX = mybir.AxisListType

D = 2048
KC = 16
EPS = 1e-6
NEG = -30000.0
N_MEM = 256
DEPTH = 4
IN_COLS = (4480, 7168, 5632)
NDMA_SEM = 24
COMPUTE = ('pe', 'act', 'dve', 'pool')


class Sched:
    def __init__(self, nc):
        self.nc = nc
        self.ops = []
        self.last_writer = {}
        self.readers = {}
        self.dma_count = {}
        self.last_on_eng = {}
        self.dma_hist = {}
        self.pending_barrier = None

    def op(self, eng, meth, reads=(), writes=(), dma=False, **kw):
        idx = len(self.ops)
        deps = set()
        for r in reads:
            w = self.last_writer.get(r)
            if w is not None:
                deps.add(w)
        for w_ in writes:
            w = self.last_writer.get(w_)
            if w is not None:
                deps.add(w)
            rd = self.readers.get(w_)
            if rd:
                deps.update(rd.values())
        for r in reads:
            self.readers.setdefault(r, {})[eng if not dma else (eng, idx)] = idx
        for w_ in writes:
            self.last_writer[w_] = idx
            self.readers[w_] = {}
        if self.pending_barrier is not None and eng not in self.pending_barrier['done']:
            deps.update(self.pending_barrier['deps'])
            self.pending_barrier['done'].add(eng)
        deps.discard(idx)
        o = dict(eng=eng, meth=meth, kw=kw, deps=deps, dma=dma, sig=False)
        if dma:
            k = self.dma_count.get(eng, 0)
            self.dma_count[eng] = k + 1
            o['dma_i'] = k
            self.dma_hist.setdefault(eng, []).append(idx)
        else:
            self.last_on_eng[eng] = idx
        self.ops.append(o)
        return idx

    def dma(self, eng, reads=(), writes=(), **kw):
        return self.op(eng, 'dma_start', reads, writes, dma=True, **kw)

    def barrier(self):
        deps = set(self.last_on_eng.values())
        for q, h in self.dma_hist.items():
            deps.update(h[-NDMA_SEM:])
        self.pending_barrier = dict(deps=deps, done=set())

    def emit(self):
        nc = self.nc
        ops = self.ops
        for o in ops:
            for d in o['deps']:
                od = ops[d]
                if od['dma']:
                    continue
                if od['eng'] != o['eng'] or o['dma'] or od['eng'] != 'pe':
                    od['sig'] = True
        cnt = {e: 0 for e in COMPUTE + ('sp',)}
        for o in ops:
            if not o['dma'] and o['sig']:
                cnt[o['eng']] += 1
                o['seq'] = cnt[o['eng']]
        engs = {'pe': nc.tensor, 'act': nc.scalar, 'dve': nc.vector, 'pool': nc.gpsimd, 'sp': nc.sync}
        with contextlib.ExitStack() as st:
            csem = {e: st.enter_context(nc.semaphore('cs_' + e)) for e in COMPUTE}
            dsem = {}
            for q in self.dma_count:
                dsem[q] = [st.enter_context(nc.semaphore('ds_%s_%d' % (q, i))) for i in range(NDMA_SEM)]
            block = st.enter_context(nc.Block())

            def body(ename):
                eng = engs[ename]
                waited = {}

                def wait(sem, key, val):
                    if waited.get(key, 0) >= val:
                        return
                    waited[key] = val
                    eng.wait_ge(sem, val)

                my_last_dma = None
                for o in ops:
                    if o['eng'] != ename:
                        continue
                    need = {}
                    for d in o['deps']:
                        od = ops[d]
                        if od['dma']:
                            q, i = od['eng'], od['dma_i']
                            k = ('d', q, i % NDMA_SEM)
                            need[k] = max(need.get(k, 0), 16 * (i // NDMA_SEM + 1))
                        else:
                            if od['eng'] == ename and not o['dma'] and ename == 'pe':
                                continue
                            k = ('c', od['eng'])
                            need[k] = max(need.get(k, 0), od['seq'])
                    for k in sorted(need):
                        sem = dsem[k[1]][k[2]] if k[0] == 'd' else csem[k[1]]
                        wait(sem, k, need[k])
                    fn = getattr(eng, o['meth'])
                    if o['dma']:
                        i = o['dma_i']
                        s = dsem[ename][i % NDMA_SEM]
                        if i >= NDMA_SEM:
                            wait(s, ('d', ename, i % NDMA_SEM), 16 * (i // NDMA_SEM))
                        fn(**o['kw']).then_inc(s, 16)
                        my_last_dma = i
                    else:
                        ins = fn(**o['kw'])
                        if o['sig']:
                            ins.then_inc(csem[ename], 1)
                if my_last_dma is not None:
                    n = my_last_dma + 1
                    for j in range(max(0, n - NDMA_SEM), n):
                        wait(dsem[ename][j % NDMA_SEM], ('d', ename, j % NDMA_SEM), 16 * (j // NDMA_SEM + 1))

            used = set(o['eng'] for o in ops)
            if 'pe' in used:
                @block.tensor
                def _(e):
                    body('pe')
            if 'act' in used:
                @block.scalar
                def _(e):
                    body('act')
            if 'dve' in used:
                @block.vector
                def _(e):
                    body('dve')
            if 'pool' in used:
                @block.gpsimd
                def _(e):
                    body('pool')
            if 'sp' in used:
                @block.sync
                def _(e):
                    body('sp')


def make_consts(S):
    NT = S // 128
    bf = ml_dtypes.bfloat16
    c = {}
    c['c_ident'] = np.eye(128, dtype=np.float32).astype(bf)
    k = np.arange(128)[:, None]
    q = np.arange(128)[None, :]
    cur = np.where(k <= q, 0.0, NEG).astype(np.float32)
    prev = np.where(k > q, 0.0, NEG).astype(np.float32)
    m = np.stack([np.tile(cur, (1, 4)), np.tile(prev, (1, 4))], axis=1)
    c['c_mask'] = m.astype(bf)
    E = np.zeros((128, 64, 128), np.float32)
    for a in range(4):
        for n in range(16):
            E[a * 32 + n, a * 16 + n, :] = -NEG
    c['c_E'] = E.astype(bf)
    gb = np.zeros((128, NT, 16), np.float32)
    for t in range(NT):
        b = t // 2
        for n in range(16):
            gb[:, t, n] = 0.0 if n < b else (1e30 if n == b else -1e30)
    c['c_gbias'] = gb
    H = 6
    T = 128
    dk = 128
    log_g = np.log1p(-np.exp(np.linspace(math.log(1.0 / 32), math.log(1.0 / 512), H).astype(np.float32).astype(np.float64)))
    i = np.arange(T, dtype=np.float64)
    diff = i[None, :] - i[:, None]
    dec = np.zeros((128, H, 128), np.float64)
    xi = np.zeros((128, H, 128), np.float64)
    zeta = np.zeros((128, H), np.float64)
    for h in range(H):
        dec[:, h, :] = np.where(diff >= 0, np.exp(np.maximum(diff, 0.0) * log_g[h]), 0.0) * dk ** -0.5
        xi[:, h, :] = np.exp((i + 1) * log_g[h])[None, :]
        zeta[:, h] = np.exp((T - 1 - i) * log_g[h]) * dk ** -0.5
    c['c_decay'] = dec.astype(np.float32)
    c['c_xi'] = xi.astype(np.float32)
    c['c_zeta'] = zeta.astype(np.float32)
    c['_gchunk'] = [float(np.exp(T * log_g[h])) for h in range(H)]
    inv = []
    for rot, theta in ((16, 500000.0), (32, 500000.0), (128, 10000.0)):
        inv.append((np.float32(theta) ** (-np.arange(0, rot, 2, dtype=np.float32) / np.float32(rot))).astype(np.float32))
    c['c_inv'] = np.tile(np.concatenate(inv)[None, :], (128, 1)).astype(np.float32)
    return c


ROPE = {'A': (0, 8, 64), 'B': (8, 16, 128), 'C': (24, 64, 128)}


def layer_blocks(mixer):
    b = []
    if mixer == 0:
        for i in range(3):
            b.append((512 * i, 512, 'qT', dict(rope='A', ft=4 * i)))
        b.append((1536, 384, 'swa_kv', dict(ft=12)))
        b.append((1920, 512, 'qT', dict(rope=None, ft=18)))
        zb = 2432
    elif mixer == 1:
        for i in range(3):
            b.append((512 * i, 512, 'qT', dict(rope='B', ft=4 * i)))
        for i in range(3):
            b.append((1536 + 512 * i, 512, 'qT', dict(rope='B', ft=12 + 4 * i)))
        for i in range(3):
            b.append((3072 + 512 * i, 512, 'tok', dict(tk=512 * i)))
        b.append((4608, 512, 'qT', dict(rope=None, ft=24)))
        zb = 5120
    else:
        b.append((0, 512, 'qT', dict(rope='C', ft=0)))
        b.append((512, 256, 'qT', dict(rope='C', ft=4)))
        b.append((768, 512, 'qT', dict(rope='C', ft=6, tk=1536)))
        b.append((1280, 256, 'qT', dict(rope='C', ft=10, tk=1536 + 512)))
        for i in range(3):
            b.append((1536 + 512 * i, 512, 'tok', dict(tk=512 * i)))
        b.append((3072, 512, 'qT', dict(rope=None, ft=12)))
        zb = 3584
    for i in range(4):
        b.append((zb + 512 * i, 512, 'z', dict(sz=512 * i)))
    return b


QM_FT = {0: 18, 1: 24, 2: 12}


def build_nc(S, depth=DEPTH, debug=False, stop_after=None, start_layer=0):
    NT = S // 128
    HALF = min(S, 2048)
    NTH = HALF // 128
    NHALF = S // HALF
    NBLK = S // 256
    consts = make_consts(S)
    gchunk = consts['_gchunk']
    nc = bass.Bass("TRN2", target_bir_lowering=False)

    def din(name, shape, dt=F32):
        return nc.dram_tensor(name, list(shape), dt, kind="ExternalInput").ap()

    x_in = din("x", [S, D])
    mem_in = din("mem", [N_MEM, D])
    pos_in = din("positions", [S], I32)
    mem_norm = din("mem_norm", [D])
    w_mem_kv = din("w_mem_kv", [D, 1024])
    norms, w_ins, w_outs, sinks = [], [], [], {}
    NORM_NAMES = ["norm_0", "norm_1", "norm_2", "norm_3"]
    WIN_NAMES = ["w_in_0", "w_in_1", "w_in_2", "w_in_3"]
    WOUT_NAMES = ["w_out_0", "w_out_1", "w_out_2", "w_out_3"]
    SINK_NAMES = {0: "sinks_0", 3: "sinks_3"}
    for l in range(DEPTH):
        if not (start_layer <= l < depth):
            norms.append(None)
            w_ins.append(None)
            w_outs.append(None)
            continue
        norms.append(din(NORM_NAMES[l], [D]))
        w_ins.append(din(WIN_NAMES[l], [D, IN_COLS[l % 3]]))
        if l % 3 == 0:
            sinks[l] = din(SINK_NAMES[l], [24])
        w_outs.append(din(WOUT_NAMES[l], [D, D]))
    final_norm = din("final_norm", [D])
    cin = {}
    for k, v in consts.items():
        if k.startswith('_'):
            continue
        cin[k] = din(k, v.shape, BF16 if v.dtype == ml_dtypes.bfloat16 else F32)
    out = nc.dram_tensor("out", [S, D], F32, kind="ExternalOutput").ap()
    skind = "ExternalOutput" if debug else "Internal"
    Hs = [nc.dram_tensor("H%d" % i, [S, D], F32, kind=skind).ap() for i in range(2)]
    FT = nc.dram_tensor("FT", [28, 128, S], BF16, kind=skind).ap()
    TK = nc.dram_tensor("TK", [S, 2304], BF16, kind=skind).ap()
    SZ = nc.dram_tensor("SZ", [S, D], BF16, kind=skind).ap()
    Y = nc.dram_tensor("Y", [S, D], BF16, kind=skind).ap()

    with contextlib.ExitStack() as st:
        def sb(name, shape, dt):
            return st.enter_context(nc.sbuf_tensor(name, list(shape), dt))

        ABF = 64 * 1024
        AF32 = 15 * 1024
        arena_bf = sb("arena_bf", [128, ABF], BF16)
        arena_f = sb("arena_f", [128, AF32], F32)
        P = [st.enter_context(nc.psum_tensor("ps%d" % i, [128, 512], F32)) for i in range(8)]
        ident = sb("ident", [128, 128], BF16)
        cmask = sb("cmask", [128, 2, 512], BF16)
        mem_kT = sb("mem_kT", [128, 4, 256], BF16)
        mem_v = sb("mem_v", [128, 2, 4, 132], BF16)
        posf = sb("posf", [128, NT], F32)
        inv_t = sb("inv_t", [128, 88], F32)
        small = sb("small", [128, 64], F32)
        esink = sb("esink", [128, 24], F32)

        sch = Sched(nc)
        uid = [0]

        class Ar:
            def __init__(self, t, n):
                self.t, self.n, self.off = t, n, 0

            def reset(self):
                self.off = 0

            def take(self, n, pat=None, **kw):
                assert self.off + n <= self.n, (self.off, n, self.n)
                ap = self.t[:, self.off:self.off + n]
                self.off += n
                if pat:
                    ap = ap.rearrange(pat, **kw)
                return ap

        abf = Ar(arena_bf, ABF)
        af = Ar(arena_f, AF32)

        def new_phase():
            sch.barrier()
            abf.reset()
            af.reset()
            uid[0] += 1
            return "p%d." % uid[0]

        small_i = [0]

        def sm(n=1):
            if small_i[0] + n > 64:
                small_i[0] = 0
            a = small_i[0]
            small_i[0] += n
            return small[:, a:a + n], ('small', a // 8), ('small', (a + n - 1) // 8)

        def psk(i):
            return ('ps', i)

        def Pbf(i):
            return P[i][:].bitcast(BF16)

        def rms_scale(pfx, xin, xkey, hbf, hkey, gt, gkey, n=D):
            ss, k0, k1 = sm(3)
            sk = list({k0, k1})
            sch.op('act', 'activation', reads=[xkey], writes=[hkey] + sk, out=hbf, in_=xin, func=AF.Square, accum_out=ss[:, 0:1])
            sch.op('act', 'activation', reads=sk, writes=sk, out=ss[:, 1:2], in_=ss[:, 0:1], func=AF.Ln, scale=1.0 / n, bias=EPS)
            sch.op('act', 'activation', reads=sk, writes=sk, out=ss[:, 2:3], in_=ss[:, 1:2], func=AF.Exp, scale=-0.5)
            sch.op('dve', 'scalar_tensor_tensor', reads=[xkey, gkey] + sk, writes=[hkey], out=hbf, in0=xin, scalar=ss[:, 2:3],
                   in1=gt, op0=ALU.mult, op1=ALU.mult)

        def transpose_to(pfx, src, skey, ncols, dst_fn, dkey_fn, bank_rot, evac_rot):
            nch = ncols // 128
            for g in range((nch + 3) // 4):
                n = min(4, nch - 4 * g)
                bk = bank_rot[0][bank_rot[1] % len(bank_rot[0])]
                bank_rot[1] += 1
                pv = Pbf(bk)
                for j in range(n):
                    cch = 4 * g + j
                    sch.op('pe', 'transpose', reads=[skey, 'ident'], writes=[psk(bk)], out=pv[:, j * 128:(j + 1) * 128],
                           in_=src[:, cch * 128:(cch + 1) * 128], identity=ident[:])
                e = evac_rot[0][evac_rot[1] % len(evac_rot[0])]
                evac_rot[1] += 1
                dst = dst_fn(g, n)
                srcv = pv[:, 0:n * 128].rearrange("p (a b) -> p a b", a=n)
                if e == 'act':
                    sch.op('act', 'copy', reads=[psk(bk)], writes=[dkey_fn(g)], out=dst, in_=srcv)
                else:
                    sch.op('dve', 'tensor_copy', reads=[psk(bk)], writes=[dkey_fn(g)], out=dst, in_=srcv)

        pfx = new_phase()
        sch.dma('sp', writes=['ident'], out=ident[:], in_=cin['c_ident'])
        sch.dma('sp', writes=['cmask'], out=cmask[:], in_=cin['c_mask'])
        sch.dma('sp', writes=['inv_t'], out=inv_t[:], in_=cin['c_inv'])
        posi = af.take(NT).bitcast(I32)
        sch.dma('sp', writes=['posi'], out=posi, in_=pos_in.rearrange("(t p) -> p t", p=128), allow_slow_non_contiguous=True)
        sch.op('dve', 'tensor_copy', reads=['posi'], writes=['posf'], out=posf[:], in_=posi)
        gt = af.take(D)
        sch.dma('sp', writes=[pfx + 'gt'], out=gt, in_=mem_norm.partition_broadcast(128))
        hmT = abf.take(KC * N_MEM, "p (k t) -> p k t", k=KC)
        wkv = [abf.take(KC * 512, "p (k c) -> p k c", k=KC) for _ in range(2)]
        wv_ = w_mem_kv.rearrange("(kc p) c -> p kc c", p=128)
        for i in range(2):
            sch.dma('pool', writes=[pfx + 'wkv%d' % i], out=wkv[i], in_=wv_[:, :, i * 512:(i + 1) * 512])
        sch.op('pool', 'memset', writes=['mem_v'], ap=mem_v[:], constant=1.0)
        for t in range(N_MEM // 128):
            xin = af.take(D)
            hb = abf.take(D)
            sch.dma('sp', writes=[pfx + 'xin%d' % t], out=xin, in_=mem_in[t * 128:(t + 1) * 128, :])
            rms_scale(pfx, xin, pfx + 'xin%d' % t, hb, pfx + 'hb%d' % t, gt, pfx + 'gt')
            transpose_to(pfx, hb, pfx + 'hb%d' % t, D,
                         lambda g, n, t=t: hmT[:, 4 * g:4 * g + n, t * 128:(t + 1) * 128],
                         lambda g: pfx + 'hmT', [[0, 1], 0], [['act', 'dve'], 0])
        for m in range(4):
            bk = 2 + m % 2
            for kc in range(KC):
                sch.op('pe', 'matmul', reads=[pfx + 'wkv0', pfx + 'hmT'], writes=[psk(bk)], out=P[bk][:, 0:256],
                       lhsT=wkv[0][:, kc, m * 128:(m + 1) * 128], rhs=hmT[:, kc, :], start=(kc == 0), stop=(kc == KC - 1))
            sch.op('act', 'copy', reads=[psk(bk)], writes=['mem_kT'], out=mem_kT[:, m, :], in_=P[bk][:, 0:256])
        for nt in range(2):
            bk = 4 + nt
            for kc in range(KC):
                sch.op('pe', 'matmul', reads=[pfx + 'wkv1', pfx + 'hmT'], writes=[psk(bk)], out=P[bk][:, :],
                       lhsT=hmT[:, kc, nt * 128:(nt + 1) * 128], rhs=wkv[1][:, kc, :], start=(kc == 0), stop=(kc == KC - 1))
            sch.op('dve', 'tensor_copy', reads=[psk(bk)], writes=['mem_v'], out=mem_v[:, nt, :, 0:128],
                   in_=P[bk][:, :].rearrange("p (m d) -> p m d", m=4))

        class _Stop(Exception):
            pass

        def phase_A(l, src):
            mixer = l % 3
            blocks = layer_blocks(mixer)
            pfx = new_phase()
            C = IN_COLS[mixer]
            Wv = w_ins[l].rearrange("(kc p) c -> p kc c", p=128)
            rope_name = 'ABC'[mixer]
            ioff, rh, hd = ROPE[rope_name]
            cosT = af.take(NT * rh, "p (t f) -> p t f", t=NT)
            sinT = af.take(NT * rh, "p (t f) -> p t f", t=NT)
            NCHK = 4 if NT % 4 == 0 else (2 if NT % 2 == 0 else 1)
            TC = NT // NCHK
            ang_ = af.take(TC * rh, "p (t f) -> p t f", t=TC)
            kf_ = af.take(TC * rh, "p (t f) -> p t f", t=TC)
            ki = kf_.bitcast(I32)
            rk = [pfx + 'rope']
            hp, _, _ = sm(1)
            sch.op('dve', 'memset', writes=[pfx + 'hpi'], ap=hp, constant=float(math.pi / 2))
            C1 = 6.28125
            C2 = float(2 * math.pi - 6.28125)
            PS = 3.1415925
            for ch in range(NCHK):
                ts_ = slice(ch * TC, (ch + 1) * TC)
                ang = ang_
                cT = cosT[:, ts_, :]
                sT = sinT[:, ts_, :]
                sch.op('dve', 'tensor_tensor', reads=['posf', 'inv_t'] + rk, writes=rk, out=ang,
                       in0=posf[:, ts_, None].to_broadcast([128, TC, rh]),
                       in1=inv_t[:, None, ioff:ioff + rh].to_broadcast([128, TC, rh]), op=ALU.mult)
                sch.op('dve', 'tensor_scalar', reads=rk, writes=rk, out=kf_, in0=ang, scalar1=float(1.0 / (2 * math.pi)), scalar2=None, op0=ALU.mult)
                sch.op('dve', 'tensor_copy', reads=rk, writes=rk, out=cT.bitcast(I32), in_=kf_)
                sch.op('dve', 'tensor_copy', reads=rk, writes=rk, out=kf_, in_=cT.bitcast(I32))
                sch.op('dve', 'scalar_tensor_tensor', reads=rk, writes=rk, out=ang, in0=kf_, scalar=-C1, in1=ang, op0=ALU.mult, op1=ALU.add)
                sch.op('dve', 'scalar_tensor_tensor', reads=rk, writes=rk, out=ang, in0=kf_, scalar=-C2, in1=ang, op0=ALU.mult, op1=ALU.add)
                sch.op('dve', 'tensor_scalar', reads=rk, writes=rk, out=ang, in0=ang, scalar1=-PS, scalar2=PS, op0=ALU.max, op1=ALU.min)
                sch.op('act', 'activation', reads=rk, writes=rk, out=sT, in_=ang, func=AF.Sin)
                sch.op('dve', 'scalar_tensor_tensor', reads=rk, writes=rk, out=ang, in0=ang, scalar=-1.0, in1=ang, op0=ALU.mult, op1=ALU.max)
                sch.op('act', 'activation', reads=rk + [pfx + 'hpi'], writes=rk, out=cT, in_=ang, func=AF.Sin, scale=-1.0, bias=hp)

            gt = af.take(D)
            sch.dma('sp', writes=[pfx + 'gt'], out=gt, in_=norms[l].partition_broadcast(128))
            NXB = 3
            xin = [af.take(D) for _ in range(NXB)]
            hT = abf.take(KC * HALF, "p (k t) -> p k t", k=KC)
            wblk = [abf.take(KC * 512, "p (k c) -> p k c", k=KC) for _ in range(2)]
            hbf = [abf.take(D) for _ in range(NXB)]
            stg = [abf.take(512) for _ in range(4)]
            fts = [abf.take(6 * 512, "p (c t) -> p c t", c=6) for _ in range(2)]
            if mixer == 0:
                kpad = [abf.take(768, "p (c r d) -> p c r d", c=3, r=2) for _ in range(2)]
                krot = [abf.take(192, "p (c d) -> p c d", c=3) for _ in range(2)]
                for i in range(2):
                    sch.op('pool', 'memset', writes=[pfx + 'kpad%d' % i], ap=kpad[i], constant=0.0)
            stg_i = [0]
            fts_i = [0]
            br_t = [[0, 1, 6, 7], 0]
            er_t = [['act', 'dve'], 0]
            acc_i = [0]

            def rope_apply(xv, dstv, nh, t, xkey, dkey):
                cs = cosT[:, t:t + 1, :].to_broadcast([128, nh, rh])
                sn = sinT[:, t:t + 1, :].to_broadcast([128, nh, rh])
                x1 = xv[:, :, 0:rh]
                x2 = xv[:, :, rh:2 * rh]
                tv = [af_tmp[i][:, 0:nh * rh].rearrange("p (a b) -> p a b", a=nh) for i in range(4)]
                keys = [pfx + 'rt%d' % i for i in range(4)]
                sch.op('dve', 'tensor_tensor', reads=[xkey, rk[0]], writes=[keys[0]], out=tv[0], in0=x1, in1=cs, op=ALU.mult)
                sch.op('dve', 'tensor_tensor', reads=[xkey, rk[0]], writes=[keys[1]], out=tv[1], in0=x2, in1=sn, op=ALU.mult)
                sch.op('dve', 'tensor_tensor', reads=[xkey, rk[0]], writes=[keys[2]], out=tv[2], in0=x2, in1=cs, op=ALU.mult)
                sch.op('dve', 'tensor_tensor', reads=[xkey, rk[0]], writes=[keys[3]], out=tv[3], in0=x1, in1=sn, op=ALU.mult)
                sch.op('pool', 'tensor_tensor', reads=[keys[0], keys[1]], writes=[dkey], out=dstv[:, :, 0:rh], in0=tv[0], in1=tv[1], op=ALU.subtract)
                sch.op('pool', 'tensor_tensor', reads=[keys[2], keys[3]], writes=[dkey], out=dstv[:, :, rh:2 * rh], in0=tv[2], in1=tv[3], op=ALU.add)

            xf = [af.take(512) for _ in range(2)]
            xf_i = [0]

            af_tmp = [af.take(256) for _ in range(4)]

            for half in range(NHALF):
                pend0 = []
                for tt in range(NTH):
                    t = half * NTH + tt
                    xi_ = xin[tt % NXB]
                    xk = pfx + 'xin%d' % (tt % NXB)
                    hb = hbf[tt % NXB]
                    hk = pfx + 'hbf%d' % (tt % NXB)
                    sch.dma('sp', writes=[xk], out=xi_, in_=src[t * 128:(t + 1) * 128, :])
                    rms_scale(pfx, xi_, xk, hb, hk, gt, pfx + 'gt')
                    if pend0:
                        pend0.pop(0)()

                    def tr0(hb=hb, hk=hk, tt=tt):
                        transpose_to(pfx, hb, hk, D,
                                     lambda g, n: hT[:, 4 * g:4 * g + n, tt * 128:(tt + 1) * 128],
                                     lambda g: (pfx + 'hT', tt), br_t, er_t)
                    pend0.append(tr0)
                while pend0:
                    pend0.pop(0)()
                if stop_after == ('A0', l):
                    raise _Stop()
                pending = []
                tile_ctr = [0]

                def wload(bi):
                    c0_, w_ = blocks[bi][0], blocks[bi][1]
                    sch.dma('pool', writes=[pfx + 'wblk%d' % (bi % 2)], out=wblk[bi % 2][:, :, 0:w_], in_=Wv[:, :, c0_:c0_ + w_])
                wload(0)
                for bi, (c0, w, kind, info) in enumerate(blocks):
                    if stop_after == ('A1', l, bi):
                        raise _Stop()
                    wb = wblk[bi % 2]
                    wk = pfx + 'wblk%d' % (bi % 2)
                    if bi + 1 < len(blocks):
                        wload(bi + 1)
                    for tt in range(NTH):
                        t = half * NTH + tt
                        bk = 2 + acc_i[0] % 4
                        acc_i[0] += 1
                        pk = psk(bk)
                        ps = P[bk][:, 0:w]
                        for kc in range(KC):
                            sch.op('pe', 'matmul', reads=[wk, (pfx + 'hT', tt)], writes=[pk], out=ps,
                                   lhsT=hT[:, kc, tt * 128:(tt + 1) * 128], rhs=wb[:, kc, 0:w], start=(kc == 0), stop=(kc == KC - 1))
                        tile_ctr[0] += 1
                        while pending and pending[0][0] <= tile_ctr[0] - 2:
                            pending.pop(0)[1]()
                        rows = slice(t * 128, (t + 1) * 128)
                        si = stg_i[0] % 4
                        stg_i[0] += 1
                        sg = stg[si]
                        sk = pfx + 'stg%d' % si
                        if kind == 'z':
                            sch.op('act', 'activation', reads=[pk], writes=[sk], out=sg[:, 0:w], in_=ps, func=AF.Silu)
                            sch.dma('sp', reads=[sk], out=SZ[rows, info['sz']:info['sz'] + w], in_=sg[:, 0:w])
                        elif kind == 'tok':
                            sch.op('act', 'copy', reads=[pk], writes=[sk], out=sg[:, 0:w], in_=ps)
                            sch.dma('sp', reads=[sk], out=TK[rows, info['tk']:info['tk'] + w], in_=sg[:, 0:w])
                        elif kind == 'qT':
                            rp = info['rope']
                            if rp is None:
                                sch.op('act', 'copy', reads=[pk], writes=[sk], out=sg[:, 0:w], in_=ps)
                            else:
                                nh = w // hd
                                xi2 = xf_i[0] % 2
                                xf_i[0] += 1
                                xfk = pfx + 'xf%d' % xi2
                                sch.op('act', 'copy', reads=[pk], writes=[xfk], out=xf[xi2][:, 0:w], in_=ps)
                                xv = xf[xi2][:, 0:w].rearrange("p (h d) -> p h d", h=nh)
                                dv = sg[:, 0:w].rearrange("p (h d) -> p h d", h=nh)
                                if 2 * rh < hd:
                                    sch.op('act', 'copy', reads=[pk], writes=[sk], out=sg[:, 0:w], in_=ps)
                                rope_apply(xv, dv, nh, t, xfk, sk)
                            if 'tk' in info:
                                sch.dma('sp', reads=[sk], out=TK[rows, info['tk']:info['tk'] + w], in_=sg[:, 0:w])
                            nch = w // 128
                            q4 = tt % 4
                            fi = fts_i[0] % 2
                            fk = pfx + 'fts%d' % fi
                            lastq = (q4 == 3 or tt == NTH - 1)
                            if lastq:
                                fts_i[0] += 1

                            def tail(sg=sg, sk=sk, w=w, fi=fi, q4=q4, fk=fk, lastq=lastq, t=t, nch=nch, ft0=info['ft']):
                                transpose_to(pfx, sg, sk, w,
                                             lambda g, n: fts[fi][:, 4 * g:4 * g + n, q4 * 128:(q4 + 1) * 128],
                                             lambda g: fk, br_t, er_t)
                                if lastq:
                                    nt4 = q4 + 1
                                    tok0 = (t - q4) * 128
                                    sch.dma('sp', reads=[fk], out=FT[ft0:ft0 + nch].rearrange("c p t -> p c t")[:, :, tok0:tok0 + nt4 * 128],
                                            in_=fts[fi][:, 0:nch, 0:nt4 * 128])
                            pending.append((tile_ctr[0], tail))
                        elif kind == 'swa_kv':
                            sch.op('act', 'copy', reads=[pk], writes=[sk], out=sg[:, 0:192], in_=P[bk][:, 192:384])
                            sch.dma('sp', reads=[sk], out=TK[rows, 0:192], in_=sg[:, 0:192])
                            ki_ = tt % 2
                            kr = krot[ki_]
                            krk = pfx + 'krot%d' % ki_
                            kp = kpad[ki_]
                            kpk = pfx + 'kpad%d' % ki_
                            xi2 = xf_i[0] % 2
                            xf_i[0] += 1
                            xfk = pfx + 'xf%d' % xi2
                            sch.op('act', 'copy', reads=[pk], writes=[xfk], out=xf[xi2][:, 0:192], in_=P[bk][:, 0:192])
                            xv = xf[xi2][:, 0:192].rearrange("p (h d) -> p h d", h=3)
                            sch.op('pool', 'tensor_copy', reads=[xfk], writes=[krk], out=kr, in_=xv)
                            rope_apply(xv, kr, 3, t, xfk, krk)
                            sch.op('pool', 'tensor_copy', reads=[krk], writes=[kpk], out=kp[:, :, 0, 0:64], in_=kr)
                            sch.op('pool', 'tensor_copy', reads=[krk], writes=[kpk], out=kp[:, :, 1, 64:128], in_=kr)
                            q4 = tt % 4
                            fi = fts_i[0] % 2
                            fk = pfx + 'fts%d' % fi
                            lastq = (q4 == 3 or tt == NTH - 1)
                            if lastq:
                                fts_i[0] += 1

                            def tail(kp=kp, kpk=kpk, fi=fi, q4=q4, fk=fk, lastq=lastq, t=t):
                                transpose_to(pfx, kp.rearrange("p c r d -> p (c r d)"), kpk, 768,
                                             lambda g, n: fts[fi][:, 4 * g:4 * g + n, q4 * 128:(q4 + 1) * 128],
                                             lambda g: fk, br_t, er_t)
                                if lastq:
                                    nt4 = q4 + 1
                                    tok0 = (t - q4) * 128
                                    sch.dma('sp', reads=[fk], out=FT[12:18].rearrange("c p t -> p c t")[:, :, tok0:tok0 + nt4 * 128],
                                            in_=fts[fi][:, 0:6, 0:nt4 * 128])
                            pending.append((tile_ctr[0], tail))
                while pending:
                    pending.pop(0)[1]()

        def finish_tile(po, pk, dcol, n, rec_src, szv, szk, yv, yk):
            r, k0, k1 = sm(1)
            sch.op('dve', 'reciprocal', reads=[pk], writes=[k0], out=r, in_=rec_src)
            sch.op('dve', 'scalar_tensor_tensor', reads=[pk, k0, szk], writes=[yk], out=yv, in0=po[:, 0:n], scalar=r, in1=szv,
                   op0=ALU.mult, op1=ALU.mult)

        def mem_attention(l, pfx):
            scale = 128 ** -0.5
            qmt = [abf.take(S) for _ in range(2)]
            szm = [abf.take(NT * 128, "p (t d) -> p t d", t=NT) for _ in range(2)]
            yb = [abf.take(NT * 128, "p (t d) -> p t d", t=NT) for _ in range(2)]
            pT = [[abf.take(512) for _ in range(2)] for _ in range(2)]
            cnt = 0
            pvc = 0
            for m in range(4):
                i = m % 2
                qk, zk, yk = pfx + 'qmt%d' % i, pfx + 'szm%d' % i, pfx + 'ybm%d' % i
                sch.dma('sp', writes=[qk], out=qmt[i], in_=FT[QM_FT[l % 3] + m])
                sch.dma('sp', writes=[zk], out=szm[i], in_=SZ[:, 1536 + m * 128:1536 + (m + 1) * 128].rearrange("(t p) d -> p t d", p=128))
                pend = None
                for rng in range(S // 512):
                    pi = cnt % 2
                    cnt += 1
                    for nt in range(2):
                        bk = nt
                        sch.op('pe', 'matmul', reads=['mem_kT', qk], writes=[psk(bk)], out=P[bk][:, :],
                               lhsT=mem_kT[:, m, nt * 128:(nt + 1) * 128], rhs=qmt[i][:, rng * 512:(rng + 1) * 512], start=True, stop=True)
                        sch.op('act', 'activation', reads=[psk(bk)], writes=[pfx + 'mpT%d%d' % (pi, nt)], out=pT[pi][nt], in_=P[bk][:, :],
                               func=AF.Exp, scale=scale)

                    def pvf(rng=rng, pi=pi, i=i, m=m, zk=zk, yk=yk):
                        nonlocal pvc
                        for qi in range(4):
                            t = rng * 4 + qi
                            bk = 2 + pvc % 4
                            pvc += 1
                            po = P[bk][:, 0:129]
                            for nt in range(2):
                                sch.op('pe', 'matmul', reads=[pfx + 'mpT%d%d' % (pi, nt), 'mem_v'], writes=[psk(bk)], out=po,
                                       lhsT=pT[pi][nt][:, qi * 128:(qi + 1) * 128], rhs=mem_v[:, nt, m, 0:129], start=(nt == 0), stop=(nt == 1))
                            finish_tile(po, psk(bk), None, 128, P[bk][:, 128:129], szm[i][:, t, :], zk, yb[i][:, t, :], yk)
                    if pend:
                        pend()
                    pend = pvf
                if pend:
                    pend()
                sch.dma('pool', reads=[yk], out=Y[:, 1536 + m * 128:1536 + (m + 1) * 128].rearrange("(t p) d -> p t d", p=128), in_=yb[i])

        def swa(l, pfx):
            scale = 64 ** -0.5
            sk_ = af.take(24)
            sch.dma('sp', writes=[pfx + 'sinks'], out=sk_, in_=sinks[l].partition_broadcast(128))
            sch.op('act', 'activation', reads=[pfx + 'sinks'], writes=['esink'], out=esink[:], in_=sk_, func=AF.Exp)
            ktp = [abf.take(S) for _ in range(2)]
            qt = [abf.take(4 * S, "p (c t) -> p c t", c=4)] * 2
            vt = [abf.take(NT * 66, "p (t d) -> p t d", t=NT) for _ in range(2)]
            szc = [abf.take(NT * 256, "p (t h e) -> p t h e", t=NT, h=4) for _ in range(2)]
            yb = [abf.take(NT * 256, "p (t h e) -> p t h e", t=NT, h=4) for _ in range(2)]
            pT = [abf.take(512) for _ in range(4)]
            tmp = [af.take(256, "p (h e) -> p h e", h=4) for _ in range(2)]
            posb = [af.take(260) for _ in range(2)]
            for i in range(2):
                sch.op('pool', 'memset', writes=[pfx + 'vt%d' % i], ap=vt[i], constant=1.0)
            SZv = SZ.rearrange("(t p) (hh e) -> p t hh e", p=128, e=64)
            Yv = Y.rearrange("(t p) (hh e) -> p t hh e", p=128, e=64)
            it = 0
            sc = 0
            pc = 0
            for c in range(3):
                for par in range(2):
                    i = it % 2
                    it += 1
                    kk_, vk, zk, yk = [pfx + n + str(i) for n in ('ktp', 'vt', 'szc', 'ybs')]
                    qk = pfx + 'qt'
                    sch.dma('sp', writes=[kk_], out=ktp[i], in_=FT[12 + c * 2 + par])
                    if par == 0:
                        sch.dma('sp', writes=[qk], out=qt[i], in_=FT[c * 4:c * 4 + 4].rearrange("c p t -> p c t"))
                    sch.dma('sp', writes=[vk], out=vt[i][:, :, 0:64], in_=TK[:, c * 64:(c + 1) * 64].rearrange("(t p) d -> p t d", p=128))
                    for hh in range(4):
                        h = c * 8 + 2 * hh + par
                        sch.dma('sp', writes=[zk], out=szc[i][:, :, hh, :], in_=SZv[:, :, h, :])
                    pend = None
                    for j in range(NT):
                        kts = ([j - 1] if j > 0 else []) + [j]
                        pts = []
                        for kt_ in kts:
                            bk = sc % 4
                            sc += 1
                            pt = pT[bk]
                            ptk = pfx + 'spT%d' % bk
                            sch.op('pe', 'matmul', reads=[kk_, qk], writes=[psk(bk)], out=P[bk][:, :],
                                   lhsT=ktp[i][:, kt_ * 128:(kt_ + 1) * 128], rhs=qt[i][:, :, j * 128:(j + 1) * 128], start=True, stop=False)
                            sch.op('pe', 'matmul', reads=['ident', 'cmask'], writes=[psk(bk)], out=P[bk][:, :],
                                   lhsT=ident[:], rhs=cmask[:, 0 if kt_ == j else 1, :], start=False, stop=True)
                            sch.op('act', 'activation', reads=[psk(bk)], writes=[ptk], out=pt, in_=P[bk][:, :], func=AF.Exp, scale=scale)
                            pts.append((pt, ptk, kt_))

                        def pvf(j=j, pts=pts, i=i, c=c, par=par, vk=vk, zk=zk, yk=yk):
                            nonlocal pc
                            bk = 4 + pc % 2
                            pc += 1
                            po = P[bk][:, 0:260].rearrange("p (h e) -> p h e", h=4)
                            for g in range(4):
                                for n_, (pt, ptk, kt_) in enumerate(pts):
                                    sch.op('pe', 'matmul', reads=[ptk, vk], writes=[psk(bk)], out=po[:, g, :],
                                           lhsT=pt[:, g * 128:(g + 1) * 128], rhs=vt[i][:, kt_, 0:65], start=(n_ == 0), stop=(n_ == len(pts) - 1))
                            pb = posb[j % 2]
                            pbk = pfx + 'posb%d' % (j % 2)
                            sch.op('dve', 'tensor_copy', reads=[psk(bk)], writes=[pbk], out=pb, in_=P[bk][:, 0:260])
                            pbv = pb.rearrange("p (h e) -> p h e", h=4)
                            den, k0, k1 = sm(8)
                            dk_ = list({k0, k1})
                            h0 = c * 8 + par
                            sch.op('dve', 'tensor_tensor', reads=[pbk, 'esink'], writes=dk_, out=den[:, 0:4], in0=pbv[:, :, 64],
                                   in1=esink[:, h0:h0 + 7:2], op=ALU.add)
                            sch.op('dve', 'reciprocal', reads=dk_, writes=dk_, out=den[:, 4:8], in_=den[:, 0:4])
                            tm = tmp[j % 2]
                            tmk = pfx + 'stmp%d' % (j % 2)
                            sch.op('dve', 'tensor_tensor', reads=[pbk] + dk_, writes=[tmk], out=tm, in0=pbv[:, :, 0:64],
                                   in1=den[:, 4:8, None].to_broadcast([128, 4, 64]), op=ALU.mult)
                            sch.op('dve', 'tensor_tensor', reads=[tmk, zk], writes=[yk], out=yb[i][:, j, :, :], in0=tm, in1=szc[i][:, j, :, :], op=ALU.mult)
                        if pend:
                            pend()
                        pend = pvf
                    if pend:
                        pend()
                    for hh in range(4):
                        h = c * 8 + 2 * hh + par
                        sch.dma('pool', reads=[yk], out=Yv[:, :, h, :], in_=yb[i][:, :, hh, :])

        def moba(l, pfx):
            scale = 128 ** -0.5
            E = abf.take(64 * 128, "p (e k) -> p e k", e=64)
            sch.dma('sp', writes=[pfx + 'E'], out=E, in_=cin['c_E'])
            gbias = af.take(NT * 16, "p (t n) -> p t n", t=NT)
            sch.dma('sp', writes=[pfx + 'gbias'], out=gbias, in_=cin['c_gbias'])
            qt = [abf.take(S) for _ in range(2)]
            kt = [abf.take(S) for _ in range(2)]
            vt = [abf.take(NT * 130, "p (t d) -> p t d", t=NT) for _ in range(2)]
            szh = [abf.take(NT * 128, "p (t d) -> p t d", t=NT) for _ in range(2)]
            yb = [abf.take(NT * 128, "p (t d) -> p t d", t=NT) for _ in range(2)]
            pT = [abf.take(256) for _ in range(4)]
            selp = abf.take(NT * 32, "p (t n) -> p t n", t=NT)
            NG = (NT + 3) // 4
            selT = abf.take(NG * 128, "p (g q) -> p g q", g=NG)
            kmh = abf.take(16)
            kml = abf.take(16)
            ksum = af.take(16)
            gm = af.take(NT * 16, "p (t n) -> p t n", t=NT)
            sel = af.take(NT * 16, "p (t n) -> p t n", t=NT)
            mx8 = af.take(NT * 8, "p (t n) -> p t n", t=NT)
            sch.op('pool', 'memset', writes=[pfx + 'selp'], ap=selp, constant=0.0)
            for i in range(2):
                sch.op('pool', 'memset', writes=[pfx + 'vt%d' % i], ap=vt[i], constant=1.0)
            sc = 0
            pvc = 0
            for h in range(12):
                i = h % 2
                qk, kk_, vk, zk, yk = [pfx + n + str(i) for n in ('qt', 'kt', 'vt', 'szh', 'ybh')]
                sch.dma('sp', writes=[qk], out=qt[i], in_=FT[h])
                sch.dma('sp', writes=[kk_], out=kt[i], in_=FT[12 + h])
                sch.dma('sp', writes=[vk], out=vt[i][:, :, 0:128], in_=TK[:, h * 128:(h + 1) * 128].rearrange("(t p) d -> p t d", p=128))
                sch.dma('sp', writes=[zk], out=szh[i], in_=SZ[:, h * 128:(h + 1) * 128].rearrange("(t p) d -> p t d", p=128))
                gk = pfx + 'gate'
                sch.op('dve', 'tensor_reduce', reads=[kk_], writes=[gk], out=ksum[:, 0:NBLK], in_=kt[i].rearrange("p (n k) -> p n k", n=NBLK),
                       axis=AX.X, op=ALU.add)
                sch.op('dve', 'tensor_scalar', reads=[gk], writes=[gk], out=ksum[:, 0:NBLK], in0=ksum[:, 0:NBLK], scalar1=1.0 / 256, scalar2=None, op0=ALU.mult)
                sch.op('dve', 'memset', writes=[gk + 'km'], ap=kmh, constant=0.0)
                sch.op('dve', 'memset', writes=[gk + 'km'], ap=kml, constant=0.0)
                sch.op('dve', 'tensor_copy', reads=[gk], writes=[gk + 'km'], out=kmh[:, 0:NBLK], in_=ksum[:, 0:NBLK])
                sch.op('dve', 'tensor_tensor', reads=[gk, gk + 'km'], writes=[gk + 'km'], out=kml[:, 0:NBLK], in0=ksum[:, 0:NBLK], in1=kmh[:, 0:NBLK], op=ALU.subtract)
                for t in range(NT):
                    sch.op('pe', 'matmul', reads=[qk, gk + 'km'], writes=[psk(6)], out=P[6][:, t * 16:(t + 1) * 16],
                           lhsT=qt[i][:, t * 128:(t + 1) * 128], rhs=kmh, start=True, stop=False)
                    sch.op('pe', 'matmul', reads=[qk, gk + 'km'], writes=[psk(6)], out=P[6][:, t * 16:(t + 1) * 16],
                           lhsT=qt[i][:, t * 128:(t + 1) * 128], rhs=kml, start=False, stop=True)
                sch.op('dve', 'tensor_tensor', reads=[psk(6), pfx + 'gbias'], writes=[gk + 'gm'], out=gm,
                       in0=P[6][:, 0:NT * 16].rearrange("p (t n) -> p t n", t=NT), in1=gbias, op=ALU.add)
                for t in range(NT):
                    sch.op('dve', 'max', reads=[gk + 'gm'], writes=[gk + 'mx'], out=mx8[:, t, :], in_=gm[:, t, :])
                sch.op('dve', 'tensor_tensor', reads=[gk + 'gm', gk + 'mx'], writes=[gk + 'sel'], out=sel, in0=gm,
                       in1=mx8[:, :, 3:4].to_broadcast([128, NT, 16]), op=ALU.is_ge)
                sch.op('dve', 'tensor_scalar', reads=[gk + 'sel'], writes=[pfx + 'selp'], out=selp[:, :, 0:16], in0=sel, scalar1=-1.0, scalar2=None, op0=ALU.add)
                pv = Pbf(6)
                selpf = selp.rearrange("p t n -> p (t n)")
                for g in range(NG):
                    n4 = min(4, NT - 4 * g)
                    sch.op('pe', 'transpose', reads=[pfx + 'selp', 'ident'], writes=[psk(6)], out=pv[0:n4 * 32, g * 128:(g + 1) * 128],
                           in_=selpf[:, g * 128:g * 128 + n4 * 32], identity=ident[:])
                if NT % 4:
                    sch.op('pool', 'memset', writes=[pfx + 'selT'], ap=selT, constant=0.0)
                    sch.op('act', 'copy', reads=[psk(6)], writes=[pfx + 'selT'], out=selT[0:(NT % 4) * 32, :, :],
                           in_=pv[0:(NT % 4) * 32, 0:NG * 128].rearrange("p (g q) -> p g q", g=NG))
                else:
                    sch.op('act', 'copy', reads=[psk(6)], writes=[pfx + 'selT'], out=selT, in_=pv[:, 0:NG * 128].rearrange("p (g q) -> p g q", g=NG))
                tasks = []
                for b in range(NBLK):
                    pob = [2 + (pvc % 2) * 2, 3 + (pvc % 2) * 2]
                    pvc += 1
                    nk = 2 * b + 2
                    for kt_ in range(nk):
                        n = kt_ // 2
                        sbk = (0, 1, 7)[sc % 3]
                        pt = pT[sc % 4]
                        ptk = pfx + 'mpT%d' % (sc % 4)
                        sc += 1

                        def score(b=b, kt_=kt_, n=n, bk=sbk, pt=pt, ptk=ptk, i=i, qk=qk, kk_=kk_):
                            ktile = kt[i][:, kt_ * 128:(kt_ + 1) * 128]
                            if kt_ < 2 * b + 1:
                                sch.op('pe', 'matmul', reads=[kk_, qk], writes=[psk(bk)], out=P[bk][:, 0:256], lhsT=ktile,
                                       rhs=qt[i][:, b * 256:(b + 1) * 256], start=True, stop=False)
                                if n < b:
                                    for sub in range(2):
                                        t = 2 * b + sub
                                        sch.op('pe', 'matmul', reads=[pfx + 'E', pfx + 'selT'], writes=[psk(bk)], out=P[bk][:, sub * 128:(sub + 1) * 128],
                                               lhsT=E[:, (t % 4) * 16 + n, :], rhs=selT[:, t // 4, :], start=False, stop=(sub == 1))
                                else:
                                    sch.op('pe', 'matmul', reads=['ident', 'cmask'], writes=[psk(bk)], out=P[bk][:, 0:128], lhsT=ident[:],
                                           rhs=cmask[:, 0, 0:128], start=False, stop=True)
                                sch.op('act', 'activation', reads=[psk(bk)], writes=[ptk], out=pt[:, 0:256], in_=P[bk][:, 0:256], func=AF.Exp, scale=scale)
                            else:
                                sch.op('pe', 'matmul', reads=[kk_, qk], writes=[psk(bk)], out=P[bk][:, 0:128], lhsT=ktile,
                                       rhs=qt[i][:, (2 * b + 1) * 128:(2 * b + 2) * 128], start=True, stop=False)
                                sch.op('pe', 'matmul', reads=['ident', 'cmask'], writes=[psk(bk)], out=P[bk][:, 0:128], lhsT=ident[:],
                                       rhs=cmask[:, 0, 0:128], start=False, stop=True)
                                sch.op('act', 'activation', reads=[psk(bk)], writes=[ptk], out=pt[:, 0:128], in_=P[bk][:, 0:128], func=AF.Exp, scale=scale)

                        def pv(b=b, kt_=kt_, pt=pt, ptk=ptk, pob=pob, i=i, vk=vk, zk=zk, yk=yk, nk=nk):
                            if kt_ < 2 * b + 1:
                                for sub in range(2):
                                    last = (kt_ == 2 * b) if sub == 0 else False
                                    sch.op('pe', 'matmul', reads=[ptk, vk], writes=[psk(pob[sub])], out=P[pob[sub]][:, 0:129],
                                           lhsT=pt[:, sub * 128:(sub + 1) * 128], rhs=vt[i][:, kt_, 0:129], start=(kt_ == 0), stop=last)
                            else:
                                sch.op('pe', 'matmul', reads=[ptk, vk], writes=[psk(pob[1])], out=P[pob[1]][:, 0:129],
                                       lhsT=pt[:, 0:128], rhs=vt[i][:, kt_, 0:129], start=False, stop=True)
                            if kt_ == nk - 1:
                                for sub in range(2):
                                    t = 2 * b + sub
                                    finish_tile(P[pob[sub]], psk(pob[sub]), None, 128, P[pob[sub]][:, 128:129], szh[i][:, t, :], zk, yb[i][:, t, :], yk)
                        tasks.append((score, pv))
                LA = 2
                for ti in range(len(tasks) + LA):
                    if ti < len(tasks):
                        tasks[ti][0]()
                    if ti - LA >= 0:
                        tasks[ti - LA][1]()
                sch.dma('pool', reads=[yk], out=Y[:, h * 128:(h + 1) * 128].rearrange("(t p) d -> p t d", p=128), in_=yb[i])

        def retention(l, pfx):
            dec = af.take(6 * 128, "p (h q) -> p h q", h=6)
            xi = af.take(6 * 128, "p (h q) -> p h q", h=6)
            zeta = af.take(6)
            sch.dma('sp', writes=[pfx + 'dec'], out=dec, in_=cin['c_decay'])
            sch.dma('sp', writes=[pfx + 'xi'], out=xi, in_=cin['c_xi'])
            sch.dma('sp', writes=[pfx + 'zeta'], out=zeta, in_=cin['c_zeta'])
            R = af.take(256)
            qt = [abf.take(S) for _ in range(2)]
            kt = [abf.take(S) for _ in range(2)]
            ktok = [abf.take(NT * 128, "p (t d) -> p t d", t=NT) for _ in range(2)]
            kz = abf.take(NT * 128, "p (t d) -> p t d", t=NT)
            vt = [abf.take(NT * 256, "p (t d) -> p t d", t=NT)] * 2
            szh = [abf.take(NT * 256, "p (t d) -> p t d", t=NT)] * 2
            yb = [abf.take(NT * 256, "p (t d) -> p t d", t=NT)] * 2
            pT = [abf.take(128) for _ in range(2)]
            qxi = [abf.take(128) for _ in range(2)]
            Rbf = [abf.take(256) for _ in range(3)]
            junk = abf.take(256)
            for h in range(6):
                i = h % 2
                qk, kk_, ktk = [pfx + n + str(i) for n in ('qt', 'kt', 'ktok')]
                vk, zk, yk = [pfx + n for n in ('vt', 'szh', 'ybh')]
                sch.dma('sp', writes=[qk], out=qt[i], in_=FT[h])
                sch.dma('sp', writes=[kk_], out=kt[i], in_=FT[6 + h])
                sch.dma('sp', writes=[ktk], out=ktok[i], in_=TK[:, 1536 + h * 128:1536 + (h + 1) * 128].rearrange("(t p) d -> p t d", p=128))
                sch.dma('sp', writes=[vk], out=vt[i], in_=TK[:, h * 256:(h + 1) * 256].rearrange("(t p) d -> p t d", p=128))
                sch.dma('sp', writes=[zk], out=szh[i], in_=SZ[:, h * 256:(h + 1) * 256].rearrange("(t p) d -> p t d", p=128))
                sch.op('act', 'activation', reads=[ktk, pfx + 'zeta'], writes=[pfx + 'kz'], out=kz, in_=ktok[i], func=AF.Copy, scale=zeta[:, h:h + 1])
                sch.op('dve', 'memset', writes=[pfx + 'R'], ap=R, constant=0.0)
                def r_score(n):
                    j = n % 2
                    sch.op('pe', 'matmul', reads=[kk_, qk], writes=[psk(j)], out=P[j][:, 0:128], lhsT=kt[i][:, n * 128:(n + 1) * 128],
                           rhs=qt[i][:, n * 128:(n + 1) * 128], start=True, stop=True)
                    sch.op('dve', 'tensor_tensor', reads=[psk(j), pfx + 'dec'], writes=[pfx + 'rpT%d' % j], out=pT[j], in0=P[j][:, 0:128], in1=dec[:, h, :], op=ALU.mult)
                    if n > 0:
                        sch.op('dve', 'tensor_tensor', reads=[qk, pfx + 'xi'], writes=[pfx + 'qxi%d' % j], out=qxi[j], in0=qt[i][:, n * 128:(n + 1) * 128],
                               in1=xi[:, h, :], op=ALU.mult)

                def r_kv(n):
                    kb = 4 + n % 2
                    sch.op('pe', 'matmul', reads=[pfx + 'kz', vk], writes=[psk(kb)], out=P[kb][:, 0:256], lhsT=kz[:, n, :], rhs=vt[i][:, n, :],
                           start=True, stop=True)
                    jn = (n + 1) % 3
                    sch.op('dve', 'scalar_tensor_tensor', reads=[pfx + 'R', psk(kb)], writes=[pfx + 'Rbf%d' % jn], out=Rbf[jn], in0=R, scalar=gchunk[h],
                           in1=P[kb][:, 0:256], op0=ALU.mult, op1=ALU.add)
                    sch.op('dve', 'scalar_tensor_tensor', reads=[pfx + 'R', psk(kb)], writes=[pfx + 'R'], out=R, in0=R, scalar=gchunk[h],
                           in1=P[kb][:, 0:256], op0=ALU.mult, op1=ALU.add)

                r_score(0)
                if NT > 1:
                    r_kv(0)
                for n in range(NT):
                    j = n % 2
                    if n + 1 < NT:
                        r_score(n + 1)
                    ob = 2 + j
                    sch.op('pe', 'matmul', reads=[pfx + 'rpT%d' % j, vk], writes=[psk(ob)], out=P[ob][:, 0:256], lhsT=pT[j], rhs=vt[i][:, n, :],
                           start=True, stop=(n == 0))
                    if n > 0:
                        sch.op('pe', 'matmul', reads=[pfx + 'qxi%d' % j, pfx + 'Rbf%d' % (n % 3)], writes=[psk(ob)], out=P[ob][:, 0:256], lhsT=qxi[j], rhs=Rbf[n % 3],
                               start=False, stop=True)
                    if n + 1 < NT - 1:
                        r_kv(n + 1)
                    ss, k0, k1 = sm(3)
                    sk2 = list({k0, k1})
                    sch.op('act', 'activation', reads=[psk(ob)], writes=[pfx + 'junk'] + sk2, out=junk, in_=P[ob][:, 0:256], func=AF.Square, accum_out=ss[:, 0:1])
                    sch.op('act', 'activation', reads=sk2, writes=sk2, out=ss[:, 1:2], in_=ss[:, 0:1], func=AF.Ln, scale=1.0 / 256, bias=EPS)
                    sch.op('act', 'activation', reads=sk2, writes=sk2, out=ss[:, 2:3], in_=ss[:, 1:2], func=AF.Exp, scale=-0.5)
                    sch.op('dve', 'scalar_tensor_tensor', reads=[psk(ob), zk] + sk2, writes=[yk], out=yb[i][:, n, :], in0=P[ob][:, 0:256], scalar=ss[:, 2:3],
                           in1=szh[i][:, n, :], op0=ALU.mult, op1=ALU.mult)
                sch.dma('pool', reads=[yk], out=Y[:, h * 256:(h + 1) * 256].rearrange("(t p) d -> p t d", p=128), in_=yb[i])

        def phase_B(l):
            pfx = new_phase()
            mixer = l % 3
            mem_attention(l, pfx)
            pfx = new_phase()
            if mixer == 0:
                swa(l, pfx)
            elif mixer == 1:
                moba(l, pfx)
            else:
                retention(l, pfx)

        def phase_C(l, src, dst, final):
            pfx = new_phase()
            wo = abf.take(KC * D, "p (k c) -> p k c", k=KC)
            Wo = w_outs[l].rearrange("(kc p) c -> p kc c", p=128)
            for cb in range(4):
                sch.dma('pool', writes=[pfx + 'wo'], out=wo[:, :, cb * 512:(cb + 1) * 512], in_=Wo[:, :, cb * 512:(cb + 1) * 512])
            yt = [abf.take(D) for _ in range(2)]
            yT = [abf.take(D, "p (k t) -> p k t", k=KC) for _ in range(2)]
            xr = [af.take(D) for _ in range(2)]
            NHO = 3 if final else 2
            ho = [af.take(D) for _ in range(NHO)]
            pendc = []
            if final:
                gf = af.take(D)
                sch.dma('sp', writes=[pfx + 'gf'], out=gf, in_=final_norm.partition_broadcast(128))
                junk = abf.take(D)
            br_t = [[0, 1], 0]
            er_t = [['act', 'dve'], 0]
            acc = 0
            for t in range(NT):
                i = t % 2
                rows = slice(t * 128, (t + 1) * 128)
                ytk, yTk, xk = [pfx + n + str(i) for n in ('yt', 'yT', 'xr')]
                hi_ = t % NHO
                hk = pfx + 'ho%d' % hi_
                sch.dma('sp', writes=[ytk], out=yt[i], in_=Y[rows, :])
                sch.dma('sp', writes=[xk], out=xr[i], in_=src[rows, :])
                transpose_to(pfx, yt[i], ytk, D, lambda g, n, i=i: yT[i][:, 4 * g:4 * g + n, :], lambda g, yTk=yTk: yTk, br_t, er_t)
                for cb in range(4):
                    bk = 2 + acc % 4
                    acc += 1
                    for kc in range(KC):
                        sch.op('pe', 'matmul', reads=[yTk, pfx + 'wo'], writes=[psk(bk)], out=P[bk][:, :], lhsT=yT[i][:, kc, :],
                               rhs=wo[:, kc, cb * 512:(cb + 1) * 512], start=(kc == 0), stop=(kc == KC - 1))
                    sch.op('dve', 'tensor_tensor', reads=[psk(bk), xk], writes=[hk], out=ho[hi_][:, cb * 512:(cb + 1) * 512], in0=P[bk][:, :],
                           in1=xr[i][:, cb * 512:(cb + 1) * 512], op=ALU.add)
                if not final:
                    sch.dma('pool', reads=[hk], out=dst[rows, :], in_=ho[hi_])
                else:
                    if pendc:
                        pendc.pop(0)()

                    def fin(hi_=hi_, hk=hk, rows=rows):
                        ss, k0, k1 = sm(3)
                        sk2 = list({k0, k1})
                        sch.op('act', 'activation', reads=[hk], writes=[pfx + 'junk'] + sk2, out=junk, in_=ho[hi_], func=AF.Square, accum_out=ss[:, 0:1])
                        sch.op('act', 'activation', reads=sk2, writes=sk2, out=ss[:, 1:2], in_=ss[:, 0:1], func=AF.Ln, scale=1.0 / D, bias=EPS)
                        sch.op('act', 'activation', reads=sk2, writes=sk2, out=ss[:, 2:3], in_=ss[:, 1:2], func=AF.Exp, scale=-0.5)
                        sch.op('dve', 'scalar_tensor_tensor', reads=[hk, pfx + 'gf'] + sk2, writes=[hk], out=ho[hi_], in0=ho[hi_], scalar=ss[:, 2:3], in1=gf,
                               op0=ALU.mult, op1=ALU.mult)
                        sch.dma('pool', reads=[hk], out=dst[rows, :], in_=ho[hi_])
                    pendc.append(fin)
            while pendc:
                pendc.pop(0)()

        src = x_in
        done = False
        for l in range(start_layer, depth if stop_after != ('P0',) else 0):
            final = (l == depth - 1)
            dst = out if final else Hs[l % 2]
            try:
                phase_A(l, src)
            except _Stop:
                break
            if stop_after == ('A', l):
                break
            phase_B(l)
            if stop_after == ('B', l):
                break
            phase_C(l, src, dst, final)
            src = dst
        sch.barrier()
        sch.emit()
    return nc, consts


_CACHE = {}


def kernel(**inputs):
    S = inputs['x'].shape[1]
    B = inputs['x'].shape[0]
    if S not in _CACHE:
        _CACHE[S] = build_nc(S)
    nc, consts = _CACHE[S]
    in_maps = []
    for core in range(8):
        b = core % B
        m = {}
        for k, v in inputs.items():
            v = np.asarray(v)
            if k in ('x', 'mem', 'positions'):
                m[k] = np.ascontiguousarray(v[b])
            else:
                m[k] = np.ascontiguousarray(v)
        for k, v in consts.items():
            if not k.startswith('_'):
                m[k] = v
        in_maps.append(m)
    res = run_bass_kernel_spmd(nc, in_maps, core_ids=list(range(8)))
    outs = [np.asarray(res.results[b]["out"], dtype=np.float32) for b in range(B)]
    return np.stack(outs, axis=0)
```

```python
import contextlib
import math
import os
import numpy as np
import ml_dtypes
import concourse.bass as bass
import concourse.mybir as mybir
from concourse.bass_utils import run_bass_kernel_spmd

F32 = mybir.dt.float32
BF16 = mybir.dt.bfloat16
I32 = mybir.dt.int32
AF = mybir.ActivationFunctionType
ALU = mybir.AluOpType
AX = mybir.AxisListType

D = 2048
KC = 16
EPS = 1e-6
NEG = -30000.0
N_MEM = 256
DEPTH = 4
IN_COLS = (4480, 7168, 5632)
NDMA_SEM = 24
COMPUTE = ('pe', 'act', 'dve', 'pool')


class Sched:
    def __init__(self, nc):
        self.nc = nc
        self.ops = []
        self.last_writer = {}
        self.readers = {}
        self.dma_count = {}
        self.last_on_eng = {}
        self.dma_hist = {}
        self.pending_barrier = None

    def op(self, eng, meth, reads=(), writes=(), dma=False, **kw):
        idx = len(self.ops)
        deps = set()
        for r in reads:
            w = self.last_writer.get(r)
            if w is not None:
                deps.add(w)
        for w_ in writes:
            w = self.last_writer.get(w_)
            if w is not None:
                deps.add(w)
            rd = self.readers.get(w_)
            if rd:
                deps.update(rd.values())
        for r in reads:
            self.readers.setdefault(r, {})[eng if not dma else (eng, idx)] = idx
        for w_ in writes:
            self.last_writer[w_] = idx
            self.readers[w_] = {}
        if self.pending_barrier is not None and eng not in self.pending_barrier['done']:
            deps.update(self.pending_barrier['deps'])
            self.pending_barrier['done'].add(eng)
        deps.discard(idx)
        o = dict(eng=eng, meth=meth, kw=kw, deps=deps, dma=dma, sig=False)
        if dma:
            k = self.dma_count.get(eng, 0)
            self.dma_count[eng] = k + 1
            o['dma_i'] = k
            self.dma_hist.setdefault(eng, []).append(idx)
        else:
            self.last_on_eng[eng] = idx
        self.ops.append(o)
        return idx

    def dma(self, eng, reads=(), writes=(), **kw):
        return self.op(eng, 'dma_start', reads, writes, dma=True, **kw)

    def barrier(self):
        deps = set(self.last_on_eng.values())
        for q, h in self.dma_hist.items():
            deps.update(h[-NDMA_SEM:])
        self.pending_barrier = dict(deps=deps, done=set())

    def emit(self):
        nc = self.nc
        ops = self.ops
        for o in ops:
            for d in o['deps']:
                od = ops[d]
                if od['dma']:
                    continue
                if od['eng'] != o['eng'] or o['dma'] or od['eng'] != 'pe':
                    od['sig'] = True
        cnt = {e: 0 for e in COMPUTE + ('sp',)}
        for o in ops:
            if not o['dma'] and o['sig']:
                cnt[o['eng']] += 1
                o['seq'] = cnt[o['eng']]
        engs = {'pe': nc.tensor, 'act': nc.scalar, 'dve': nc.vector, 'pool': nc.gpsimd, 'sp': nc.sync}
        with contextlib.ExitStack() as st:
            csem = {e: st.enter_context(nc.semaphore('cs_' + e)) for e in COMPUTE}
            dsem = {}
            for q in self.dma_count:
                dsem[q] = [st.enter_context(nc.semaphore('ds_%s_%d' % (q, i))) for i in range(NDMA_SEM)]
            block = st.enter_context(nc.Block())

            def body(ename):
                eng = engs[ename]
                waited = {}

                def wait(sem, key, val):
                    if waited.get(key, 0) >= val:
                        return
                    waited[key] = val
                    eng.wait_ge(sem, val)

                my_last_dma = None
                for o in ops:
                    if o['eng'] != ename:
                        continue
                    need = {}
                    for d in o['deps']:
                        od = ops[d]
                        if od['dma']:
                            q, i = od['eng'], od['dma_i']
                            k = ('d', q, i % NDMA_SEM)
                            need[k] = max(need.get(k, 0), 16 * (i // NDMA_SEM + 1))
                        else:
                            if od['eng'] == ename and not o['dma'] and ename == 'pe':
                                continue
                            k = ('c', od['eng'])
                            need[k] = max(need.get(k, 0), od['seq'])
                    for k in sorted(need):
                        sem = dsem[k[1]][k[2]] if k[0] == 'd' else csem[k[1]]
                        wait(sem, k, need[k])
                    fn = getattr(eng, o['meth'])
                    if o['dma']:
                        i = o['dma_i']
                        s = dsem[ename][i % NDMA_SEM]
                        if i >= NDMA_SEM:
                            wait(s, ('d', ename, i % NDMA_SEM), 16 * (i // NDMA_SEM))
                        fn(**o['kw']).then_inc(s, 16)
                        my_last_dma = i
                    else:
                        ins = fn(**o['kw'])
                        if o['sig']:
                            ins.then_inc(csem[ename], 1)
                if my_last_dma is not None:
                    n = my_last_dma + 1
                    for j in range(max(0, n - NDMA_SEM), n):
                        wait(dsem[ename][j % NDMA_SEM], ('d', ename, j % NDMA_SEM), 16 * (j // NDMA_SEM + 1))

            used = set(o['eng'] for o in ops)
            if 'pe' in used:
                @block.tensor
                def _(e):
                    body('pe')
            if 'act' in used:
                @block.scalar
                def _(e):
                    body('act')
            if 'dve' in used:
                @block.vector
                def _(e):
                    body('dve')
            if 'pool' in used:
                @block.gpsimd
                def _(e):
                    body('pool')
            if 'sp' in used:
                @block.sync
                def _(e):
                    body('sp')


def make_consts(S):
    NT = S // 128
    bf = ml_dtypes.bfloat16
    c = {}
    c['c_ident'] = np.eye(128, dtype=np.float32).astype(bf)
    k = np.arange(128)[:, None]
    q = np.arange(128)[None, :]
    cur = np.where(k <= q, 0.0, NEG).astype(np.float32)
    prev = np.where(k > q, 0.0, NEG).astype(np.float32)
    m = np.stack([np.tile(cur, (1, 4)), np.tile(prev, (1, 4))], axis=1)
    c['c_mask'] = m.astype(bf)
    E = np.zeros((128, 64, 128), np.float32)
    for a in range(4):
        for n in range(16):
            E[a * 32 + n, a * 16 + n, :] = -NEG
    c['c_E'] = E.astype(bf)
    gb = np.zeros((128, NT, 16), np.float32)
    for t in range(NT):
        b = t // 2
        for n in range(16):
            gb[:, t, n] = 0.0 if n < b else (1e30 if n == b else -1e30)
    c['c_gbias'] = gb
    H = 6
    T = 128
    dk = 128
    log_g = np.log1p(-np.exp(np.linspace(math.log(1.0 / 32), math.log(1.0 / 512), H).astype(np.float32).astype(np.float64)))
    i = np.arange(T, dtype=np.float64)
    diff = i[None, :] - i[:, None]
    dec = np.zeros((128, H, 128), np.float64)
    xi = np.zeros((128, H, 128), np.float64)
    zeta = np.zeros((128, H), np.float64)
    for h in range(H):
        dec[:, h, :] = np.where(diff >= 0, np.exp(np.maximum(diff, 0.0) * log_g[h]), 0.0) * dk ** -0.5
        xi[:, h, :] = np.exp((i + 1) * log_g[h])[None, :]
        zeta[:, h] = np.exp((T - 1 - i) * log_g[h]) * dk ** -0.5
    c['c_decay'] = dec.astype(np.float32)
    c['c_xi'] = xi.astype(np.float32)
    c['c_zeta'] = zeta.astype(np.float32)
    c['_gchunk'] = [float(np.exp(T * log_g[h])) for h in range(H)]
    inv = []
    for rot, theta in ((16, 500000.0), (32, 500000.0), (128, 10000.0)):
        inv.append((np.float32(theta) ** (-np.arange(0, rot, 2, dtype=np.float32) / np.float32(rot))).astype(np.float32))
    c['c_inv'] = np.tile(np.concatenate(inv)[None, :], (128, 1)).astype(np.float32)
    return c


ROPE = {'A': (0, 8, 64), 'B': (8, 16, 128), 'C': (24, 64, 128)}


def layer_blocks(mixer):
    b = []
    if mixer == 0:
        for i in range(3):
            b.append((512 * i, 512, 'qT', dict(rope='A', ft=4 * i)))
        b.append((1536, 384, 'swa_kv', dict(ft=12)))
        b.append((1920, 512, 'qT', dict(rope=None, ft=18)))
        zb = 2432
    elif mixer == 1:
        for i in range(3):
            b.append((512 * i, 512, 'qT', dict(rope='B', ft=4 * i)))
        for i in range(3):
            b.append((1536 + 512 * i, 512, 'qT', dict(rope='B', ft=12 + 4 * i)))
        for i in range(3):
            b.append((3072 + 512 * i, 512, 'tok', dict(tk=512 * i)))
        b.append((4608, 512, 'qT', dict(rope=None, ft=24)))
        zb = 5120
    else:
        b.append((0, 512, 'qT', dict(rope='C', ft=0)))
        b.append((512, 256, 'qT', dict(rope='C', ft=4)))
        b.append((768, 512, 'qT', dict(rope='C', ft=6, tk=1536)))
        b.append((1280, 256, 'qT', dict(rope='C', ft=10, tk=1536 + 512)))
        for i in range(3):
            b.append((1536 + 512 * i, 512, 'tok', dict(tk=512 * i)))
        b.append((3072, 512, 'qT', dict(rope=None, ft=12)))
        zb = 3584
    for i in range(4):
        b.append((zb + 512 * i, 512, 'z', dict(sz=512 * i)))
    return b


QM_FT = {0: 18, 1: 24, 2: 12}


def build_nc(S, depth=DEPTH, debug=False, stop_after=None, start_layer=0):
    NT = S // 128
    HALF = min(S, 2048)
    NTH = HALF // 128
    NHALF = S // HALF
    NBLK = S // 256
    consts = make_consts(S)
    gchunk = consts['_gchunk']
    nc = bass.Bass("TRN2", target_bir_lowering=False)

    def din(name, shape, dt=F32):
        return nc.dram_tensor(name, list(shape), dt, kind="ExternalInput").ap()

    x_in = din("x", [S, D])
    mem_in = din("mem", [N_MEM, D])
    pos_in = din("positions", [S], I32)
    mem_norm = din("mem_norm", [D])
    w_mem_kv = din("w_mem_kv", [D, 1024])
    norms, w_ins, w_outs, sinks = [], [], [], {}
    NORM_NAMES = ["norm_0", "norm_1", "norm_2", "norm_3"]
    WIN_NAMES = ["w_in_0", "w_in_1", "w_in_2", "w_in_3"]
    WOUT_NAMES = ["w_out_0", "w_out_1", "w_out_2", "w_out_3"]
    SINK_NAMES = {0: "sinks_0", 3: "sinks_3"}
    for l in range(DEPTH):
        if not (start_layer <= l < depth):
            norms.append(None)
            w_ins.append(None)
            w_outs.append(None)
            continue
        norms.append(din(NORM_NAMES[l], [D]))
        w_ins.append(din(WIN_NAMES[l], [D, IN_COLS[l % 3]]))
        if l % 3 == 0:
            sinks[l] = din(SINK_NAMES[l], [24])
        w_outs.append(din(WOUT_NAMES[l], [D, D]))
    final_norm = din("final_norm", [D])
    cin = {}
    for k, v in consts.items():
        if k.startswith('_'):
            continue
        cin[k] = din(k, v.shape, BF16 if v.dtype == ml_dtypes.bfloat16 else F32)
    out = nc.dram_tensor("out", [S, D], F32, kind="ExternalOutput").ap()
    skind = "ExternalOutput" if debug else "Internal"
    Hs = [nc.dram_tensor("H%d" % i, [S, D], F32, kind=skind).ap() for i in range(2)]
    FT = nc.dram_tensor("FT", [28, 128, S], BF16, kind=skind).ap()
    TK = nc.dram_tensor("TK", [S, 2304], BF16, kind=skind).ap()
    SZ = nc.dram_tensor("SZ", [S, D], BF16, kind=skind).ap()
    Y = nc.dram_tensor("Y", [S, D], BF16, kind=skind).ap()

    with contextlib.ExitStack() as st:
        def sb(name, shape, dt):
            return st.enter_context(nc.sbuf_tensor(name, list(shape), dt))

        ABF = 64 * 1024
        AF32 = 15 * 1024
        arena_bf = sb("arena_bf", [128, ABF], BF16)
        arena_f = sb("arena_f", [128, AF32], F32)
        P = [st.enter_context(nc.psum_tensor("ps%d" % i, [128, 512], F32)) for i in range(8)]
        ident = sb("ident", [128, 128], BF16)
        cmask = sb("cmask", [128, 2, 512], BF16)
        mem_kT = sb("mem_kT", [128, 4, 256], BF16)
        mem_v = sb("mem_v", [128, 2, 4, 132], BF16)
        posf = sb("posf", [128, NT], F32)
        inv_t = sb("inv_t", [128, 88], F32)
        small = sb("small", [128, 64], F32)
        esink = sb("esink", [128, 24], F32)

        sch = Sched(nc)
        uid = [0]

        class Ar:
            def __init__(self, t, n):
                self.t, self.n, self.off = t, n, 0

            def reset(self):
                self.off = 0

            def take(self, n, pat=None, **kw):
                assert self.off + n <= self.n, (self.off, n, self.n)
                ap = self.t[:, self.off:self.off + n]
                self.off += n
                if pat:
                    ap = ap.rearrange(pat, **kw)
                return ap

        abf = Ar(arena_bf, ABF)
        af = Ar(arena_f, AF32)

        def new_phase():
            sch.barrier()
            abf.reset()
            af.reset()
            uid[0] += 1
            return "p%d." % uid[0]

        small_i = [0]

        def sm(n=1):
            if small_i[0] + n > 64:
                small_i[0] = 0
            a = small_i[0]
            small_i[0] += n
            return small[:, a:a + n], ('small', a // 8), ('small', (a + n - 1) // 8)

        def psk(i):
            return ('ps', i)

        def Pbf(i):
            return P[i][:].bitcast(BF16)

        def rms_scale(pfx, xin, xkey, hbf, hkey, gt, gkey, n=D):
            ss, k0, k1 = sm(3)
            sk = list({k0, k1})
            sch.op('act', 'activation', reads=[xkey], writes=[hkey] + sk, out=hbf, in_=xin, func=AF.Square, accum_out=ss[:, 0:1])
            sch.op('act', 'activation', reads=sk, writes=sk, out=ss[:, 1:2], in_=ss[:, 0:1], func=AF.Ln, scale=1.0 / n, bias=EPS)
            sch.op('act', 'activation', reads=sk, writes=sk, out=ss[:, 2:3], in_=ss[:, 1:2], func=AF.Exp, scale=-0.5)
            sch.op('dve', 'scalar_tensor_tensor', reads=[xkey, gkey] + sk, writes=[hkey], out=hbf, in0=xin, scalar=ss[:, 2:3],
                   in1=gt, op0=ALU.mult, op1=ALU.mult)

        def transpose_to(pfx, src, skey, ncols, dst_fn, dkey_fn, bank_rot, evac_rot):
            nch = ncols // 128
            for g in range((nch + 3) // 4):
                n = min(4, nch - 4 * g)
                bk = bank_rot[0][bank_rot[1] % len(bank_rot[0])]
                bank_rot[1] += 1
                pv = Pbf(bk)
                for j in range(n):
                    cch = 4 * g + j
                    sch.op('pe', 'transpose', reads=[skey, 'ident'], writes=[psk(bk)], out=pv[:, j * 128:(j + 1) * 128],
                           in_=src[:, cch * 128:(cch + 1) * 128], identity=ident[:])
                e = evac_rot[0][evac_rot[1] % len(evac_rot[0])]
                evac_rot[1] += 1
                dst = dst_fn(g, n)
                srcv = pv[:, 0:n * 128].rearrange("p (a b) -> p a b", a=n)
                if e == 'act':
                    sch.op('act', 'copy', reads=[psk(bk)], writes=[dkey_fn(g)], out=dst, in_=srcv)
                else:
                    sch.op('dve', 'tensor_copy', reads=[psk(bk)], writes=[dkey_fn(g)], out=dst, in_=srcv)

        pfx = new_phase()
        sch.dma('sp', writes=['ident'], out=ident[:], in_=cin['c_ident'])
        sch.dma('sp', writes=['cmask'], out=cmask[:], in_=cin['c_mask'])
        sch.dma('sp', writes=['inv_t'], out=inv_t[:], in_=cin['c_inv'])
        posi = af.take(NT).bitcast(I32)
        sch.dma('sp', writes=['posi'], out=posi, in_=pos_in.rearrange("(t p) -> p t", p=128), allow_slow_non_contiguous=True)
        sch.op('dve', 'tensor_copy', reads=['posi'], writes=['posf'], out=posf[:], in_=posi)
        gt = af.take(D)
        sch.dma('sp', writes=[pfx + 'gt'], out=gt, in_=mem_norm.partition_broadcast(128))
        hmT = abf.take(KC * N_MEM, "p (k t) -> p k t", k=KC)
        wkv = [abf.take(KC * 512, "p (k c) -> p k c", k=KC) for _ in range(2)]
        wv_ = w_mem_kv.rearrange("(kc p) c -> p kc c", p=128)
        for i in range(2):
            sch.dma('pool', writes=[pfx + 'wkv%d' % i], out=wkv[i], in_=wv_[:, :, i * 512:(i + 1) * 512])
        sch.op('pool', 'memset', writes=['mem_v'], ap=mem_v[:], constant=1.0)
        for t in range(N_MEM // 128):
            xin = af.take(D)
            hb = abf.take(D)
            sch.dma('sp', writes=[pfx + 'xin%d' % t], out=xin, in_=mem_in[t * 128:(t + 1) * 128, :])
            rms_scale(pfx, xin, pfx + 'xin%d' % t, hb, pfx + 'hb%d' % t, gt, pfx + 'gt')
            transpose_to(pfx, hb, pfx + 'hb%d' % t, D,
                         lambda g, n, t=t: hmT[:, 4 * g:4 * g + n, t * 128:(t + 1) * 128],
                         lambda g: pfx + 'hmT', [[0, 1], 0], [['act', 'dve'], 0])
        for m in range(4):
            bk = 2 + m % 2
            for kc in range(KC):
                sch.op('pe', 'matmul', reads=[pfx + 'wkv0', pfx + 'hmT'], writes=[psk(bk)], out=P[bk][:, 0:256],
                       lhsT=wkv[0][:, kc, m * 128:(m + 1) * 128], rhs=hmT[:, kc, :], start=(kc == 0), stop=(kc == KC - 1))
            sch.op('act', 'copy', reads=[psk(bk)], writes=['mem_kT'], out=mem_kT[:, m, :], in_=P[bk][:, 0:256])
        for nt in range(2):
            bk = 4 + nt
            for kc in range(KC):
                sch.op('pe', 'matmul', reads=[pfx + 'wkv1', pfx + 'hmT'], writes=[psk(bk)], out=P[bk][:, :],
                       lhsT=hmT[:, kc, nt * 128:(nt + 1) * 128], rhs=wkv[1][:, kc, :], start=(kc == 0), stop=(kc == KC - 1))
            sch.op('dve', 'tensor_copy', reads=[psk(bk)], writes=['mem_v'], out=mem_v[:, nt, :, 0:128],
                   in_=P[bk][:, :].rearrange("p (m d) -> p m d", m=4))

        class _Stop(Exception):
            pass

        def phase_A(l, src):
            mixer = l % 3
            blocks = layer_blocks(mixer)
            pfx = new_phase()
            C = IN_COLS[mixer]
            Wv = w_ins[l].rearrange("(kc p) c -> p kc c", p=128)
            rope_name = 'ABC'[mixer]
            ioff, rh, hd = ROPE[rope_name]
            cosT = af.take(NT * rh, "p (t f) -> p t f", t=NT)
            sinT = af.take(NT * rh, "p (t f) -> p t f", t=NT)
            NCHK = 4 if NT % 4 == 0 else (2 if NT % 2 == 0 else 1)
            TC = NT // NCHK
            ang_ = af.take(TC * rh, "p (t f) -> p t f", t=TC)
            kf_ = af.take(TC * rh, "p (t f) -> p t f", t=TC)
            ki = kf_.bitcast(I32)
            rk = [pfx + 'rope']
            hp, _, _ = sm(1)
            sch.op('dve', 'memset', writes=[pfx + 'hpi'], ap=hp, constant=float(math.pi / 2))
            C1 = 6.28125
            C2 = float(2 * math.pi - 6.28125)
            PS = 3.1415925
            for ch in range(NCHK):
                ts_ = slice(ch * TC, (ch + 1) * TC)
                ang = ang_
                cT = cosT[:, ts_, :]
                sT = sinT[:, ts_, :]
                sch.op('dve', 'tensor_tensor', reads=['posf', 'inv_t'] + rk, writes=rk, out=ang,
                       in0=posf[:, ts_, None].to_broadcast([128, TC, rh]),
                       in1=inv_t[:, None, ioff:ioff + rh].to_broadcast([128, TC, rh]), op=ALU.mult)
                sch.op('dve', 'tensor_scalar', reads=rk, writes=rk, out=kf_, in0=ang, scalar1=float(1.0 / (2 * math.pi)), scalar2=None, op0=ALU.mult)
                sch.op('dve', 'tensor_copy', reads=rk, writes=rk, out=cT.bitcast(I32), in_=kf_)
                sch.op('dve', 'tensor_copy', reads=rk, writes=rk, out=kf_, in_=cT.bitcast(I32))
                sch.op('dve', 'scalar_tensor_tensor', reads=rk, writes=rk, out=ang, in0=kf_, scalar=-C1, in1=ang, op0=ALU.mult, op1=ALU.add)
                sch.op('dve', 'scalar_tensor_tensor', reads=rk, writes=rk, out=ang, in0=kf_, scalar=-C2, in1=ang, op0=ALU.mult, op1=ALU.add)
                sch.op('dve', 'tensor_scalar', reads=rk, writes=rk, out=ang, in0=ang, scalar1=-PS, scalar2=PS, op0=ALU.max, op1=ALU.min)
                sch.op('act', 'activation', reads=rk, writes=rk, out=sT, in_=ang, func=AF.Sin)
                sch.op('dve', 'scalar_tensor_tensor', reads=rk, writes=rk, out=ang, in0=ang, scalar=-1.0, in1=ang, op0=ALU.mult, op1=ALU.max)
                sch.op('act', 'activation', reads=rk + [pfx + 'hpi'], writes=rk, out=cT, in_=ang, func=AF.Sin, scale=-1.0, bias=hp)

            gt = af.take(D)
            sch.dma('sp', writes=[pfx + 'gt'], out=gt, in_=norms[l].partition_broadcast(128))
            NXB = 3
            xin = [af.take(D) for _ in range(NXB)]
            hT = abf.take(KC * HALF, "p (k t) -> p k t", k=KC)
            wblk = [abf.take(KC * 512, "p (k c) -> p k c", k=KC) for _ in range(2)]
            hbf = [abf.take(D) for _ in range(NXB)]
            stg = [abf.take(512) for _ in range(4)]
            fts = [abf.take(6 * 512, "p (c t) -> p c t", c=6) for _ in range(2)]
            if mixer == 0:
                kpad = [abf.take(768, "p (c r d) -> p c r d", c=3, r=2) for _ in range(2)]
                krot = [abf.take(192, "p (c d) -> p c d", c=3) for _ in range(2)]
                for i in range(2):
                    sch.op('pool', 'memset', writes=[pfx + 'kpad%d' % i], ap=kpad[i], constant=0.0)
            stg_i = [0]
            fts_i = [0]
            br_t = [[0, 1, 6, 7], 0]
            er_t = [['act', 'dve'], 0]
            acc_i = [0]

            def rope_apply(xv, dstv, nh, t, xkey, dkey):
                cs = cosT[:, t:t + 1, :].to_broadcast([128, nh, rh])
                sn = sinT[:, t:t + 1, :].to_broadcast([128, nh, rh])
                x1 = xv[:, :, 0:rh]
                x2 = xv[:, :, rh:2 * rh]
                tv = [af_tmp[i][:, 0:nh * rh].rearrange("p (a b) -> p a b", a=nh) for i in range(4)]
                keys = [pfx + 'rt%d' % i for i in range(4)]
                sch.op('dve', 'tensor_tensor', reads=[xkey, rk[0]], writes=[keys[0]], out=tv[0], in0=x1, in1=cs, op=ALU.mult)
                sch.op('dve', 'tensor_tensor', reads=[xkey, rk[0]], writes=[keys[1]], out=tv[1], in0=x2, in1=sn, op=ALU.mult)
                sch.op('dve', 'tensor_tensor', reads=[xkey, rk[0]], writes=[keys[2]], out=tv[2], in0=x2, in1=cs, op=ALU.mult)
                sch.op('dve', 'tensor_tensor', reads=[xkey, rk[0]], writes=[keys[3]], out=tv[3], in0=x1, in1=sn, op=ALU.mult)
                sch.op('pool', 'tensor_tensor', reads=[keys[0], keys[1]], writes=[dkey], out=dstv[:, :, 0:rh], in0=tv[0], in1=tv[1], op=ALU.subtract)
                sch.op('pool', 'tensor_tensor', reads=[keys[2], keys[3]], writes=[dkey], out=dstv[:, :, rh:2 * rh], in0=tv[2], in1=tv[3], op=ALU.add)

            xf = [af.take(512) for _ in range(2)]
            xf_i = [0]

            af_tmp = [af.take(256) for _ in range(4)]

            for half in range(NHALF):
                pend0 = []
                for tt in range(NTH):
                    t = half * NTH + tt
                    xi_ = xin[tt % NXB]
                    xk = pfx + 'xin%d' % (tt % NXB)
                    hb = hbf[tt % NXB]
                    hk = pfx + 'hbf%d' % (tt % NXB)
                    sch.dma('sp', writes=[xk], out=xi_, in_=src[t * 128:(t + 1) * 128, :])
                    rms_scale(pfx, xi_, xk, hb, hk, gt, pfx + 'gt')
                    if pend0:
                        pend0.pop(0)()

                    def tr0(hb=hb, hk=hk, tt=tt):
                        transpose_to(pfx, hb, hk, D,
                                     lambda g, n: hT[:, 4 * g:4 * g + n, tt * 128:(tt + 1) * 128],
                                     lambda g: (pfx + 'hT', tt), br_t, er_t)
                    pend0.append(tr0)
                while pend0:
                    pend0.pop(0)()
                if stop_after == ('A0', l):
                    raise _Stop()
                pending = []
                tile_ctr = [0]

                def wload(bi):
                    c0_, w_ = blocks[bi][0], blocks[bi][1]
                    sch.dma('pool', writes=[pfx + 'wblk%d' % (bi % 2)], out=wblk[bi % 2][:, :, 0:w_], in_=Wv[:, :, c0_:c0_ + w_])
                wload(0)
                for bi, (c0, w, kind, info) in enumerate(blocks):
                    if stop_after == ('A1', l, bi):
                        raise _Stop()
                    wb = wblk[bi % 2]
                    wk = pfx + 'wblk%d' % (bi % 2)
                    if bi + 1 < len(blocks):
                        wload(bi + 1)
                    for tt in range(NTH):
                        t = half * NTH + tt
                        bk = 2 + acc_i[0] % 4
                        acc_i[0] += 1
                        pk = psk(bk)
                        ps = P[bk][:, 0:w]
                        for kc in range(KC):
                            sch.op('pe', 'matmul', reads=[wk, (pfx + 'hT', tt)], writes=[pk], out=ps,
                                   lhsT=hT[:, kc, tt * 128:(tt + 1) * 128], rhs=wb[:, kc, 0:w], start=(kc == 0), stop=(kc == KC - 1))
                        tile_ctr[0] += 1
                        while pending and pending[0][0] <= tile_ctr[0] - 2:
                            pending.pop(0)[1]()
                        rows = slice(t * 128, (t + 1) * 128)
                        si = stg_i[0] % 4
                        stg_i[0] += 1
                        sg = stg[si]
                        sk = pfx + 'stg%d' % si
                        if kind == 'z':
                            sch.op('act', 'activation', reads=[pk], writes=[sk], out=sg[:, 0:w], in_=ps, func=AF.Silu)
                            sch.dma('sp', reads=[sk], out=SZ[rows, info['sz']:info['sz'] + w], in_=sg[:, 0:w])
                        elif kind == 'tok':
                            sch.op('act', 'copy', reads=[pk], writes=[sk], out=sg[:, 0:w], in_=ps)
                            sch.dma('sp', reads=[sk], out=TK[rows, info['tk']:info['tk'] + w], in_=sg[:, 0:w])
                        elif kind == 'qT':
                            rp = info['rope']
                            if rp is None:
                                sch.op('act', 'copy', reads=[pk], writes=[sk], out=sg[:, 0:w], in_=ps)
                            else:
                                nh = w // hd
                                xi2 = xf_i[0] % 2
                                xf_i[0] += 1
                                xfk = pfx + 'xf%d' % xi2
                                sch.op('act', 'copy', reads=[pk], writes=[xfk], out=xf[xi2][:, 0:w], in_=ps)
                                xv = xf[xi2][:, 0:w].rearrange("p (h d) -> p h d", h=nh)
                                dv = sg[:, 0:w].rearrange("p (h d) -> p h d", h=nh)
                                if 2 * rh < hd:
                                    sch.op('act', 'copy', reads=[pk], writes=[sk], out=sg[:, 0:w], in_=ps)
                                rope_apply(xv, dv, nh, t, xfk, sk)
                            if 'tk' in info:
                                sch.dma('sp', reads=[sk], out=TK[rows, info['tk']:info['tk'] + w], in_=sg[:, 0:w])
                            nch = w // 128
                            q4 = tt % 4
                            fi = fts_i[0] % 2
                            fk = pfx + 'fts%d' % fi
                            lastq = (q4 == 3 or tt == NTH - 1)
                            if lastq:
                                fts_i[0] += 1

                            def tail(sg=sg, sk=sk, w=w, fi=fi, q4=q4, fk=fk, lastq=lastq, t=t, nch=nch, ft0=info['ft']):
                                transpose_to(pfx, sg, sk, w,
                                             lambda g, n: fts[fi][:, 4 * g:4 * g + n, q4 * 128:(q4 + 1) * 128],
                                             lambda g: fk, br_t, er_t)
                                if lastq:
                                    nt4 = q4 + 1
                                    tok0 = (t - q4) * 128
                                    sch.dma('sp', reads=[fk], out=FT[ft0:ft0 + nch].rearrange("c p t -> p c t")[:, :, tok0:tok0 + nt4 * 128],
                                            in_=fts[fi][:, 0:nch, 0:nt4 * 128])
                            pending.append((tile_ctr[0], tail))
                        elif kind == 'swa_kv':
                            sch.op('act', 'copy', reads=[pk], writes=[sk], out=sg[:, 0:192], in_=P[bk][:, 192:384])
                            sch.dma('sp', reads=[sk], out=TK[rows, 0:192], in_=sg[:, 0:192])
                            ki_ = tt % 2
                            kr = krot[ki_]
                            krk = pfx + 'krot%d' % ki_
                            kp = kpad[ki_]
                            kpk = pfx + 'kpad%d' % ki_
                            xi2 = xf_i[0] % 2
                            xf_i[0] += 1
                            xfk = pfx + 'xf%d' % xi2
                            sch.op('act', 'copy', reads=[pk], writes=[xfk], out=xf[xi2][:, 0:192], in_=P[bk][:, 0:192])
                            xv = xf[xi2][:, 0:192].rearrange("p (h d) -> p h d", h=3)
                            sch.op('pool', 'tensor_copy', reads=[xfk], writes=[krk], out=kr, in_=xv)
                            rope_apply(xv, kr, 3, t, xfk, krk)
                            sch.op('pool', 'tensor_copy', reads=[krk], writes=[kpk], out=kp[:, :, 0, 0:64], in_=kr)
                            sch.op('pool', 'tensor_copy', reads=[krk], writes=[kpk], out=kp[:, :, 1, 64:128], in_=kr)
                            q4 = tt % 4
                            fi = fts_i[0] % 2
                            fk = pfx + 'fts%d' % fi
                            lastq = (q4 == 3 or tt == NTH - 1)
                            if lastq:
                                fts_i[0] += 1

                            def tail(kp=kp, kpk=kpk, fi=fi, q4=q4, fk=fk, lastq=lastq, t=t):
                                transpose_to(pfx, kp.rearrange("p c r d -> p (c r d)"), kpk, 768,
                                             lambda g, n: fts[fi][:, 4 * g:4 * g + n, q4 * 128:(q4 + 1) * 128],
                                             lambda g: fk, br_t, er_t)
                                if lastq:
                                    nt4 = q4 + 1
                                    tok0 = (t - q4) * 128
                                    sch.dma('sp', reads=[fk], out=FT[12:18].rearrange("c p t -> p c t")[:, :, tok0:tok0 + nt4 * 128],
                                            in_=fts[fi][:, 0:6, 0:nt4 * 128])
                            pending.append((tile_ctr[0], tail))
                while pending:
                    pending.pop(0)[1]()

        def finish_tile(po, pk, dcol, n, rec_src, szv, szk, yv, yk):
            r, k0, k1 = sm(1)
            sch.op('dve', 'reciprocal', reads=[pk], writes=[k0], out=r, in_=rec_src)
            sch.op('dve', 'scalar_tensor_tensor', reads=[pk, k0, szk], writes=[yk], out=yv, in0=po[:, 0:n], scalar=r, in1=szv,
                   op0=ALU.mult, op1=ALU.mult)

        def mem_attention(l, pfx):
            scale = 128 ** -0.5
            qmt = [abf.take(S) for _ in range(2)]
            szm = [abf.take(NT * 128, "p (t d) -> p t d", t=NT) for _ in range(2)]
            yb = [abf.take(NT * 128, "p (t d) -> p t d", t=NT) for _ in range(2)]
            pT = [[abf.take(512) for _ in range(2)] for _ in range(2)]
            cnt = 0
            pvc = 0
            for m in range(4):
                i = m % 2
                qk, zk, yk = pfx + 'qmt%d' % i, pfx + 'szm%d' % i, pfx + 'ybm%d' % i
                sch.dma('sp', writes=[qk], out=qmt[i], in_=FT[QM_FT[l % 3] + m])
                sch.dma('sp', writes=[zk], out=szm[i], in_=SZ[:, 1536 + m * 128:1536 + (m + 1) * 128].rearrange("(t p) d -> p t d", p=128))
                pend = None
                for rng in range(S // 512):
                    pi = cnt % 2
                    cnt += 1
                    for nt in range(2):
                        bk = nt
                        sch.op('pe', 'matmul', reads=['mem_kT', qk], writes=[psk(bk)], out=P[bk][:, :],
                               lhsT=mem_kT[:, m, nt * 128:(nt + 1) * 128], rhs=qmt[i][:, rng * 512:(rng + 1) * 512], start=True, stop=True)
                        sch.op('act', 'activation', reads=[psk(bk)], writes=[pfx + 'mpT%d%d' % (pi, nt)], out=pT[pi][nt], in_=P[bk][:, :],
                               func=AF.Exp, scale=scale)

                    def pvf(rng=rng, pi=pi, i=i, m=m, zk=zk, yk=yk):
                        nonlocal pvc
                        for qi in range(4):
                            t = rng * 4 + qi
                            bk = 2 + pvc % 4
                            pvc += 1
                            po = P[bk][:, 0:129]
                            for nt in range(2):
                                sch.op('pe', 'matmul', reads=[pfx + 'mpT%d%d' % (pi, nt), 'mem_v'], writes=[psk(bk)], out=po,
                                       lhsT=pT[pi][nt][:, qi * 128:(qi + 1) * 128], rhs=mem_v[:, nt, m, 0:129], start=(nt == 0), stop=(nt == 1))
                            finish_tile(po, psk(bk), None, 128, P[bk][:, 128:129], szm[i][:, t, :], zk, yb[i][:, t, :], yk)
                    if pend:
                        pend()
                    pend = pvf
                if pend:
                    pend()
                sch.dma('pool', reads=[yk], out=Y[:, 1536 + m * 128:1536 + (m + 1) * 128].rearrange("(t p) d -> p t d", p=128), in_=yb[i])

        def swa(l, pfx):
            scale = 64 ** -0.5
            sk_ = af.take(24)
            sch.dma('sp', writes=[pfx + 'sinks'], out=sk_, in_=sinks[l].partition_broadcast(128))
            sch.op('act', 'activation', reads=[pfx + 'sinks'], writes=['esink'], out=esink[:], in_=sk_, func=AF.Exp)
            ktp = [abf.take(S) for _ in range(2)]
            qt = [abf.take(4 * S, "p (c t) -> p c t", c=4)] * 2
            vt = [abf.take(NT * 66, "p (t d) -> p t d", t=NT) for _ in range(2)]
            szc = [abf.take(NT * 256, "p (t h e) -> p t h e", t=NT, h=4) for _ in range(2)]
            yb = [abf.take(NT * 256, "p (t h e) -> p t h e", t=NT, h=4) for _ in range(2)]
            pT = [abf.take(512) for _ in range(4)]
            tmp = [af.take(256, "p (h e) -> p h e", h=4) for _ in range(2)]
            posb = [af.take(260) for _ in range(2)]
            for i in range(2):
                sch.op('pool', 'memset', writes=[pfx + 'vt%d' % i], ap=vt[i], constant=1.0)
            SZv = SZ.rearrange("(t p) (hh e) -> p t hh e", p=128, e=64)
            Yv = Y.rearrange("(t p) (hh e) -> p t hh e", p=128, e=64)
            it = 0
            sc = 0
            pc = 0
            for c in range(3):
                for par in range(2):
                    i = it % 2
                    it += 1
                    kk_, vk, zk, yk = [pfx + n + str(i) for n in ('ktp', 'vt', 'szc', 'ybs')]
                    qk = pfx + 'qt'
                    sch.dma('sp', writes=[kk_], out=ktp[i], in_=FT[12 + c * 2 + par])
                    if par == 0:
                        sch.dma('sp', writes=[qk], out=qt[i], in_=FT[c * 4:c * 4 + 4].rearrange("c p t -> p c t"))
                    sch.dma('sp', writes=[vk], out=vt[i][:, :, 0:64], in_=TK[:, c * 64:(c + 1) * 64].rearrange("(t p) d -> p t d", p=128))
                    for hh in range(4):
                        h = c * 8 + 2 * hh + par
                        sch.dma('sp', writes=[zk], out=szc[i][:, :, hh, :], in_=SZv[:, :, h, :])
                    pend = None
                    for j in range(NT):
                        kts = ([j - 1] if j > 0 else []) + [j]
                        pts = []
                        for kt_ in kts:
                            bk = sc % 4
                            sc += 1
                            pt = pT[bk]
                            ptk = pfx + 'spT%d' % bk
                            sch.op('pe', 'matmul', reads=[kk_, qk], writes=[psk(bk)], out=P[bk][:, :],
                                   lhsT=ktp[i][:, kt_ * 128:(kt_ + 1) * 128], rhs=qt[i][:, :, j * 128:(j + 1) * 128], start=True, stop=False)
                            sch.op('pe', 'matmul', reads=['ident', 'cmask'], writes=[psk(bk)], out=P[bk][:, :],
                                   lhsT=ident[:], rhs=cmask[:, 0 if kt_ == j else 1, :], start=False, stop=True)
                            sch.op('act', 'activation', reads=[psk(bk)], writes=[ptk], out=pt, in_=P[bk][:, :], func=AF.Exp, scale=scale)
                            pts.append((pt, ptk, kt_))

                        def pvf(j=j, pts=pts, i=i, c=c, par=par, vk=vk, zk=zk, yk=yk):
                            nonlocal pc
                            bk = 4 + pc % 2
                            pc += 1
                            po = P[bk][:, 0:260].rearrange("p (h e) -> p h e", h=4)
                            for g in range(4):
                                for n_, (pt, ptk, kt_) in enumerate(pts):
                                    sch.op('pe', 'matmul', reads=[ptk, vk], writes=[psk(bk)], out=po[:, g, :],
                                           lhsT=pt[:, g * 128:(g + 1) * 128], rhs=vt[i][:, kt_, 0:65], start=(n_ == 0), stop=(n_ == len(pts) - 1))
                            pb = posb[j % 2]
                            pbk = pfx + 'posb%d' % (j % 2)
                            sch.op('act', 'copy', reads=[psk(bk)], writes=[pbk], out=pb, in_=P[bk][:, 0:260])
                            pbv = pb.rearrange("p (h e) -> p h e", h=4)
                            den, k0, k1 = sm(8)
                            dk_ = list({k0, k1})
                            h0 = c * 8 + par
                            sch.op('dve', 'tensor_tensor', reads=[pbk, 'esink'], writes=dk_, out=den[:, 0:4], in0=pbv[:, :, 64],
                                   in1=esink[:, h0:h0 + 7:2], op=ALU.add)
                            sch.op('dve', 'reciprocal', reads=dk_, writes=dk_, out=den[:, 4:8], in_=den[:, 0:4])
                            tm = tmp[j % 2]
                            tmk = pfx + 'stmp%d' % (j % 2)
                            sch.op('dve', 'tensor_tensor', reads=[pbk] + dk_, writes=[tmk], out=tm, in0=pbv[:, :, 0:64],
                                   in1=den[:, 4:8, None].to_broadcast([128, 4, 64]), op=ALU.mult)
                            sch.op('dve', 'tensor_tensor', reads=[tmk, zk], writes=[yk], out=yb[i][:, j, :, :], in0=tm, in1=szc[i][:, j, :, :], op=ALU.mult)
                        if pend:
                            pend()
                        pend = pvf
                    if pend:
                        pend()
                    for hh in range(4):
                        h = c * 8 + 2 * hh + par
                        sch.dma('pool', reads=[yk], out=Yv[:, :, h, :], in_=yb[i][:, :, hh, :])

        def moba(l, pfx):
            scale = 128 ** -0.5
            E = abf.take(64 * 128, "p (e k) -> p e k", e=64)
            sch.dma('sp', writes=[pfx + 'E'], out=E, in_=cin['c_E'])
            gbias = af.take(NT * 16, "p (t n) -> p t n", t=NT)
            sch.dma('sp', writes=[pfx + 'gbias'], out=gbias, in_=cin['c_gbias'])
            qt = [abf.take(S) for _ in range(2)]
            kt = [abf.take(S) for _ in range(2)]
            vt = [abf.take(NT * 130, "p (t d) -> p t d", t=NT) for _ in range(2)]
            szh = [abf.take(NT * 128, "p (t d) -> p t d", t=NT) for _ in range(2)]
            yb = [abf.take(NT * 128, "p (t d) -> p t d", t=NT) for _ in range(2)]
            pT = [abf.take(256) for _ in range(4)]
            selp = abf.take(NT * 32, "p (t n) -> p t n", t=NT)
            NG = (NT + 3) // 4
            selT = abf.take(NG * 128, "p (g q) -> p g q", g=NG)
            kmh = abf.take(16)
            kml = abf.take(16)
            ksum = af.take(16)
            gm = af.take(NT * 16, "p (t n) -> p t n", t=NT)
            sel = af.take(NT * 16, "p (t n) -> p t n", t=NT)
            mx8 = af.take(NT * 8, "p (t n) -> p t n", t=NT)
            sch.op('pool', 'memset', writes=[pfx + 'selp'], ap=selp, constant=0.0)
            for i in range(2):
                sch.op('pool', 'memset', writes=[pfx + 'vt%d' % i], ap=vt[i], constant=1.0)
            sc = 0
            pvc = 0
            for h in range(12):
                i = h % 2
                qk, kk_, vk, zk, yk = [pfx + n + str(i) for n in ('qt', 'kt', 'vt', 'szh', 'ybh')]
                sch.dma('sp', writes=[qk], out=qt[i], in_=FT[h])
                sch.dma('sp', writes=[kk_], out=kt[i], in_=FT[12 + h])
                sch.dma('sp', writes=[vk], out=vt[i][:, :, 0:128], in_=TK[:, h * 128:(h + 1) * 128].rearrange("(t p) d -> p t d", p=128))
                sch.dma('sp', writes=[zk], out=szh[i], in_=SZ[:, h * 128:(h + 1) * 128].rearrange("(t p) d -> p t d", p=128))
                gk = pfx + 'gate'
                sch.op('dve', 'tensor_reduce', reads=[kk_], writes=[gk], out=ksum[:, 0:NBLK], in_=kt[i].rearrange("p (n k) -> p n k", n=NBLK),
                       axis=AX.X, op=ALU.add)
                sch.op('dve', 'tensor_scalar', reads=[gk], writes=[gk], out=ksum[:, 0:NBLK], in0=ksum[:, 0:NBLK], scalar1=1.0 / 256, scalar2=None, op0=ALU.mult)
                sch.op('dve', 'memset', writes=[gk + 'km'], ap=kmh, constant=0.0)
                sch.op('dve', 'memset', writes=[gk + 'km'], ap=kml, constant=0.0)
                sch.op('dve', 'tensor_copy', reads=[gk], writes=[gk + 'km'], out=kmh[:, 0:NBLK], in_=ksum[:, 0:NBLK])
                sch.op('dve', 'tensor_tensor', reads=[gk, gk + 'km'], writes=[gk + 'km'], out=kml[:, 0:NBLK], in0=ksum[:, 0:NBLK], in1=kmh[:, 0:NBLK], op=ALU.subtract)
                for t in range(NT):
                    sch.op('pe', 'matmul', reads=[qk, gk + 'km'], writes=[psk(6)], out=P[6][:, t * 16:(t + 1) * 16],
                           lhsT=qt[i][:, t * 128:(t + 1) * 128], rhs=kmh, start=True, stop=False)
                    sch.op('pe', 'matmul', reads=[qk, gk + 'km'], writes=[psk(6)], out=P[6][:, t * 16:(t + 1) * 16],
                           lhsT=qt[i][:, t * 128:(t + 1) * 128], rhs=kml, start=False, stop=True)
                sch.op('dve', 'tensor_tensor', reads=[psk(6), pfx + 'gbias'], writes=[gk + 'gm'], out=gm,
                       in0=P[6][:, 0:NT * 16].rearrange("p (t n) -> p t n", t=NT), in1=gbias, op=ALU.add)
                for t in range(NT):
                    sch.op('dve', 'max', reads=[gk + 'gm'], writes=[gk + 'mx'], out=mx8[:, t, :], in_=gm[:, t, :])
                sch.op('dve', 'tensor_tensor', reads=[gk + 'gm', gk + 'mx'], writes=[gk + 'sel'], out=sel, in0=gm,
                       in1=mx8[:, :, 3:4].to_broadcast([128, NT, 16]), op=ALU.is_ge)
                sch.op('dve', 'tensor_scalar', reads=[gk + 'sel'], writes=[pfx + 'selp'], out=selp[:, :, 0:16], in0=sel, scalar1=-1.0, scalar2=None, op0=ALU.add)
                pv = Pbf(6)
                selpf = selp.rearrange("p t n -> p (t n)")
                for g in range(NG):
                    n4 = min(4, NT - 4 * g)
                    sch.op('pe', 'transpose', reads=[pfx + 'selp', 'ident'], writes=[psk(6)], out=pv[0:n4 * 32, g * 128:(g + 1) * 128],
                           in_=selpf[:, g * 128:g * 128 + n4 * 32], identity=ident[:])
                if NT % 4:
                    sch.op('pool', 'memset', writes=[pfx + 'selT'], ap=selT, constant=0.0)
                    sch.op('act', 'copy', reads=[psk(6)], writes=[pfx + 'selT'], out=selT[0:(NT % 4) * 32, :, :],
                           in_=pv[0:(NT % 4) * 32, 0:NG * 128].rearrange("p (g q) -> p g q", g=NG))
                else:
                    sch.op('act', 'copy', reads=[psk(6)], writes=[pfx + 'selT'], out=selT, in_=pv[:, 0:NG * 128].rearrange("p (g q) -> p g q", g=NG))
                tasks = []
                for b in range(NBLK):
                    pob = [2 + (pvc % 2) * 2, 3 + (pvc % 2) * 2]
                    pvc += 1
                    nk = 2 * b + 2
                    for kt_ in range(nk):
                        n = kt_ // 2
                        sbk = (0, 1, 7)[sc % 3]
                        pt = pT[sc % 4]
                        ptk = pfx + 'mpT%d' % (sc % 4)
                        sc += 1

                        def score(b=b, kt_=kt_, n=n, bk=sbk, pt=pt, ptk=ptk, i=i, qk=qk, kk_=kk_):
                            ktile = kt[i][:, kt_ * 128:(kt_ + 1) * 128]
                            if kt_ < 2 * b + 1:
                                sch.op('pe', 'matmul', reads=[kk_, qk], writes=[psk(bk)], out=P[bk][:, 0:256], lhsT=ktile,
                                       rhs=qt[i][:, b * 256:(b + 1) * 256], start=True, stop=False)
                                if n < b:
                                    for sub in range(2):
                                        t = 2 * b + sub
                                        sch.op('pe', 'matmul', reads=[pfx + 'E', pfx + 'selT'], writes=[psk(bk)], out=P[bk][:, sub * 128:(sub + 1) * 128],
                                               lhsT=E[:, (t % 4) * 16 + n, :], rhs=selT[:, t // 4, :], start=False, stop=(sub == 1))
                                else:
                                    sch.op('pe', 'matmul', reads=['ident', 'cmask'], writes=[psk(bk)], out=P[bk][:, 0:128], lhsT=ident[:],
                                           rhs=cmask[:, 0, 0:128], start=False, stop=True)
                                sch.op('act', 'activation', reads=[psk(bk)], writes=[ptk], out=pt[:, 0:256], in_=P[bk][:, 0:256], func=AF.Exp, scale=scale)
                            else:
                                sch.op('pe', 'matmul', reads=[kk_, qk], writes=[psk(bk)], out=P[bk][:, 0:128], lhsT=ktile,
                                       rhs=qt[i][:, (2 * b + 1) * 128:(2 * b + 2) * 128], start=True, stop=False)
                                sch.op('pe', 'matmul', reads=['ident', 'cmask'], writes=[psk(bk)], out=P[bk][:, 0:128], lhsT=ident[:],
                                       rhs=cmask[:, 0, 0:128], start=False, stop=True)
                                sch.op('act', 'activation', reads=[psk(bk)], writes=[ptk], out=pt[:, 0:128], in_=P[bk][:, 0:128], func=AF.Exp, scale=scale)

                        def pv(b=b, kt_=kt_, pt=pt, ptk=ptk, pob=pob, i=i, vk=vk, zk=zk, yk=yk, nk=nk):
                            if kt_ < 2 * b + 1:
                                for sub in range(2):
                                    last = (kt_ == 2 * b) if sub == 0 else False
                                    sch.op('pe', 'matmul', reads=[ptk, vk], writes=[psk(pob[sub])], out=P[pob[sub]][:, 0:129],
                                           lhsT=pt[:, sub * 128:(sub + 1) * 128], rhs=vt[i][:, kt_, 0:129], start=(kt_ == 0), stop=last)
                            else:
                                sch.op('pe', 'matmul', reads=[ptk, vk], writes=[psk(pob[1])], out=P[pob[1]][:, 0:129],
                                       lhsT=pt[:, 0:128], rhs=vt[i][:, kt_, 0:129], start=False, stop=True)
                            if kt_ == nk - 1:
                                for sub in range(2):
                                    t = 2 * b + sub
                                    finish_tile(P[pob[sub]], psk(pob[sub]), None, 128, P[pob[sub]][:, 128:129], szh[i][:, t, :], zk, yb[i][:, t, :], yk)
                        tasks.append((score, pv))
                LA = 2
                for ti in range(len(tasks) + LA):
                    if ti < len(tasks):
                        tasks[ti][0]()
                    if ti - LA >= 0:
                        tasks[ti - LA][1]()
                sch.dma('pool', reads=[yk], out=Y[:, h * 128:(h + 1) * 128].rearrange("(t p) d -> p t d", p=128), in_=yb[i])

        def retention(l, pfx):
            dec = af.take(6 * 128, "p (h q) -> p h q", h=6)
            xi = af.take(6 * 128, "p (h q) -> p h q", h=6)
            zeta = af.take(6)
            sch.dma('sp', writes=[pfx + 'dec'], out=dec, in_=cin['c_decay'])
            sch.dma('sp', writes=[pfx + 'xi'], out=xi, in_=cin['c_xi'])
            sch.dma('sp', writes=[pfx + 'zeta'], out=zeta, in_=cin['c_zeta'])
            R = af.take(256)
            qt = [abf.take(S) for _ in range(2)]
            kt = [abf.take(S) for _ in range(2)]
            ktok = [abf.take(NT * 128, "p (t d) -> p t d", t=NT) for _ in range(2)]
            kz = abf.take(NT * 128, "p (t d) -> p t d", t=NT)
            vt = [abf.take(NT * 256, "p (t d) -> p t d", t=NT)] * 2
            szh = [abf.take(NT * 256, "p (t d) -> p t d", t=NT)] * 2
            yb = [abf.take(NT * 256, "p (t d) -> p t d", t=NT)] * 2
            pT = [abf.take(128) for _ in range(2)]
            qxi = [abf.take(128) for _ in range(2)]
            Rbf = [abf.take(256) for _ in range(3)]
            junk = abf.take(256)
            for h in range(6):
                i = h % 2
                qk, kk_, ktk = [pfx + n + str(i) for n in ('qt', 'kt', 'ktok')]
                vk, zk, yk = [pfx + n for n in ('vt', 'szh', 'ybh')]
                sch.dma('sp', writes=[qk], out=qt[i], in_=FT[h])
                sch.dma('sp', writes=[kk_], out=kt[i], in_=FT[6 + h])
                sch.dma('sp', writes=[ktk], out=ktok[i], in_=TK[:, 1536 + h * 128:1536 + (h + 1) * 128].rearrange("(t p) d -> p t d", p=128))
                sch.dma('sp', writes=[vk], out=vt[i], in_=TK[:, h * 256:(h + 1) * 256].rearrange("(t p) d -> p t d", p=128))
                sch.dma('sp', writes=[zk], out=szh[i], in_=SZ[:, h * 256:(h + 1) * 256].rearrange("(t p) d -> p t d", p=128))
                sch.op('act', 'activation', reads=[ktk, pfx + 'zeta'], writes=[pfx + 'kz'], out=kz, in_=ktok[i], func=AF.Copy, scale=zeta[:, h:h + 1])
                sch.op('dve', 'memset', writes=[pfx + 'R'], ap=R, constant=0.0)
                def r_score(n):
                    j = n % 2
                    sch.op('pe', 'matmul', reads=[kk_, qk], writes=[psk(j)], out=P[j][:, 0:128], lhsT=kt[i][:, n * 128:(n + 1) * 128],
                           rhs=qt[i][:, n * 128:(n + 1) * 128], start=True, stop=True)
                    sch.op('dve', 'tensor_tensor', reads=[psk(j), pfx + 'dec'], writes=[pfx + 'rpT%d' % j], out=pT[j], in0=P[j][:, 0:128], in1=dec[:, h, :], op=ALU.mult)
                    if n > 0:
                        sch.op('dve', 'tensor_tensor', reads=[qk, pfx + 'xi'], writes=[pfx + 'qxi%d' % j], out=qxi[j], in0=qt[i][:, n * 128:(n + 1) * 128],
                               in1=xi[:, h, :], op=ALU.mult)

                def r_kv(n):
                    kb = 4 + n % 2
                    sch.op('pe', 'matmul', reads=[pfx + 'kz', vk], writes=[psk(kb)], out=P[kb][:, 0:256], lhsT=kz[:, n, :], rhs=vt[i][:, n, :],
                           start=True, stop=True)
                    jn = (n + 1) % 3
                    sch.op('dve', 'scalar_tensor_tensor', reads=[pfx + 'R', psk(kb)], writes=[pfx + 'Rbf%d' % jn], out=Rbf[jn], in0=R, scalar=gchunk[h],
                           in1=P[kb][:, 0:256], op0=ALU.mult, op1=ALU.add)
                    sch.op('dve', 'scalar_tensor_tensor', reads=[pfx + 'R', psk(kb)], writes=[pfx + 'R'], out=R, in0=R, scalar=gchunk[h],
                           in1=P[kb][:, 0:256], op0=ALU.mult, op1=ALU.add)

                r_score(0)
                if NT > 1:
                    r_kv(0)
                for n in range(NT):
                    j = n % 2
                    if n + 1 < NT:
                        r_score(n + 1)
                    ob = 2 + j
                    sch.op('pe', 'matmul', reads=[pfx + 'rpT%d' % j, vk], writes=[psk(ob)], out=P[ob][:, 0:256], lhsT=pT[j], rhs=vt[i][:, n, :],
                           start=True, stop=(n == 0))
                    if n > 0:
                        sch.op('pe', 'matmul', reads=[pfx + 'qxi%d' % j, pfx + 'Rbf%d' % (n % 3)], writes=[psk(ob)], out=P[ob][:, 0:256], lhsT=qxi[j], rhs=Rbf[n % 3],
                               start=False, stop=True)
                    if n + 1 < NT - 1:
                        r_kv(n + 1)
                    ss, k0, k1 = sm(3)
                    sk2 = list({k0, k1})
                    sch.op('act', 'activation', reads=[psk(ob)], writes=[pfx + 'junk'] + sk2, out=junk, in_=P[ob][:, 0:256], func=AF.Square, accum_out=ss[:, 0:1])
                    sch.op('act', 'activation', reads=sk2, writes=sk2, out=ss[:, 1:2], in_=ss[:, 0:1], func=AF.Ln, scale=1.0 / 256, bias=EPS)
                    sch.op('act', 'activation', reads=sk2, writes=sk2, out=ss[:, 2:3], in_=ss[:, 1:2], func=AF.Exp, scale=-0.5)
                    sch.op('dve', 'scalar_tensor_tensor', reads=[psk(ob), zk] + sk2, writes=[yk], out=yb[i][:, n, :], in0=P[ob][:, 0:256], scalar=ss[:, 2:3],
                           in1=szh[i][:, n, :], op0=ALU.mult, op1=ALU.mult)
                sch.dma('pool', reads=[yk], out=Y[:, h * 256:(h + 1) * 256].rearrange("(t p) d -> p t d", p=128), in_=yb[i])

        def phase_B(l):
            pfx = new_phase()
            mixer = l % 3
            mem_attention(l, pfx)
            pfx = new_phase()
            if mixer == 0:
                swa(l, pfx)
            elif mixer == 1:
                moba(l, pfx)
            else:
                retention(l, pfx)

        def phase_C(l, src, dst, final):
            pfx = new_phase()
            wo = abf.take(KC * D, "p (k c) -> p k c", k=KC)
            Wo = w_outs[l].rearrange("(kc p) c -> p kc c", p=128)
            for cb in range(4):
                sch.dma('pool', writes=[pfx + 'wo'], out=wo[:, :, cb * 512:(cb + 1) * 512], in_=Wo[:, :, cb * 512:(cb + 1) * 512])
            yt = [abf.take(D) for _ in range(2)]
            yT = [abf.take(D, "p (k t) -> p k t", k=KC) for _ in range(2)]
            xr = [af.take(D) for _ in range(2)]
            NHO = 3 if final else 2
            ho = [af.take(D) for _ in range(NHO)]
            pendc = []
            if final:
                gf = af.take(D)
                sch.dma('sp', writes=[pfx + 'gf'], out=gf, in_=final_norm.partition_broadcast(128))
                junk = abf.take(D)
            br_t = [[0, 1], 0]
            er_t = [['act', 'dve'], 0]
            acc = 0
            for t in range(NT):
                i = t % 2
                rows = slice(t * 128, (t + 1) * 128)
                ytk, yTk, xk = [pfx + n + str(i) for n in ('yt', 'yT', 'xr')]
                hi_ = t % NHO
                hk = pfx + 'ho%d' % hi_
                sch.dma('sp', writes=[ytk], out=yt[i], in_=Y[rows, :])
                sch.dma('sp', writes=[xk], out=xr[i], in_=src[rows, :])
                transpose_to(pfx, yt[i], ytk, D, lambda g, n, i=i: yT[i][:, 4 * g:4 * g + n, :], lambda g, yTk=yTk: yTk, br_t, er_t)
                for cb in range(4):
                    bk = 2 + acc % 4
                    acc += 1
                    for kc in range(KC):
                        sch.op('pe', 'matmul', reads=[yTk, pfx + 'wo'], writes=[psk(bk)], out=P[bk][:, :], lhsT=yT[i][:, kc, :],
                               rhs=wo[:, kc, cb * 512:(cb + 1) * 512], start=(kc == 0), stop=(kc == KC - 1))
                    sch.op('dve', 'tensor_tensor', reads=[psk(bk), xk], writes=[hk], out=ho[hi_][:, cb * 512:(cb + 1) * 512], in0=P[bk][:, :],
                           in1=xr[i][:, cb * 512:(cb + 1) * 512], op=ALU.add)
                if not final:
                    sch.dma('pool', reads=[hk], out=dst[rows, :], in_=ho[hi_])
                else:
                    if pendc:
                        pendc.pop(0)()

                    def fin(hi_=hi_, hk=hk, rows=rows):
                        ss, k0, k1 = sm(3)
                        sk2 = list({k0, k1})
                        sch.op('act', 'activation', reads=[hk], writes=[pfx + 'junk'] + sk2, out=junk, in_=ho[hi_], func=AF.Square, accum_out=ss[:, 0:1])
                        sch.op('act', 'activation', reads=sk2, writes=sk2, out=ss[:, 1:2], in_=ss[:, 0:1], func=AF.Ln, scale=1.0 / D, bias=EPS)
                        sch.op('act', 'activation', reads=sk2, writes=sk2, out=ss[:, 2:3], in_=ss[:, 1:2], func=AF.Exp, scale=-0.5)
                        sch.op('dve', 'scalar_tensor_tensor', reads=[hk, pfx + 'gf'] + sk2, writes=[hk], out=ho[hi_], in0=ho[hi_], scalar=ss[:, 2:3], in1=gf,
                               op0=ALU.mult, op1=ALU.mult)
                        sch.dma('pool', reads=[hk], out=dst[rows, :], in_=ho[hi_])
                    pendc.append(fin)
            while pendc:
                pendc.pop(0)()

        src = x_in
        done = False
        for l in range(start_layer, depth if stop_after != ('P0',) else 0):
            final = (l == depth - 1)
            dst = out if final else Hs[l % 2]
            try:
                phase_A(l, src)
            except _Stop:
                break
            if stop_after == ('A', l):
                break
            phase_B(l)
            if stop_after == ('B', l):
                break
            phase_C(l, src, dst, final)
            src = dst
        sch.barrier()
        sch.emit()
    return nc, consts


_CACHE = {}


def kernel(**inputs):
    S = inputs['x'].shape[1]
    B = inputs['x'].shape[0]
    if S not in _CACHE:
        _CACHE[S] = build_nc(S)
    nc, consts = _CACHE[S]
    in_maps = []
    for core in range(8):
        b = core % B
        m = {}
        for k, v in inputs.items():
            v = np.asarray(v)
            if k in ('x', 'mem', 'positions'):
                m[k] = np.ascontiguousarray(v[b])
            else:
                m[k] = np.ascontiguousarray(v)
        for k, v in consts.items():
            if not k.startswith('_'):
                m[k] = v
        in_maps.append(m)
    res = run_bass_kernel_spmd(nc, in_maps, core_ids=list(range(8)))
    outs = [np.asarray(res.results[b]["out"], dtype=np.float32) for b in range(B)]
    return np.stack(outs, axis=0)
```
